# Optimizing a Trainium2 kernel written in Bass

```python
import math
import jax
import jax.numpy as jnp
from jax import lax
import numpy as np

D_MODEL = 4096
BATCH = 4
SEQ = 4096
DEPTH = 2

N_A_LAYERS = DEPTH // 2
N_B_LAYERS = DEPTH - N_A_LAYERS
PLE_DIM = 256
RMS_EPS = 1e-6
ROPE_THETA = 500000.0
ROPE_FRACTION = 4
NEG_BIG = -1e30
FORCE_SCORE = 1e9

NSA_HEAD_DIM = 128
NSA_HEADS = D_MODEL // NSA_HEAD_DIM
NSA_KV_GROUPS = 4
NSA_HPG = NSA_HEADS // NSA_KV_GROUPS
NSA_WIDTH = NSA_HEADS * NSA_HEAD_DIM
NSA_KV_WIDTH = NSA_KV_GROUPS * NSA_HEAD_DIM
N_BRANCH = 3
CMP_LEN = 32
CMP_STRIDE = 16
CMP_HIDDEN = 256
SEL_BLOCK = 64
SEL_TOPK = 16
SEL_LOCAL = 2
NSA_WINDOW = 512
NSA_Q_BLOCK = 32
A_SIZES = (NSA_WIDTH,) + (NSA_KV_WIDTH,) * 6 + (N_BRANCH * NSA_HEADS, N_BRANCH * NSA_WIDTH)
A_IN = sum(A_SIZES)

SWA_HEAD_DIM = 64
SWA_HEADS = D_MODEL // SWA_HEAD_DIM
SWA_KV_HEADS = 8
SWA_HPG = SWA_HEADS // SWA_KV_HEADS
SWA_WIDTH = SWA_HEADS * SWA_HEAD_DIM
SWA_KV_WIDTH = SWA_KV_HEADS * SWA_HEAD_DIM
SWA_WINDOW = 128
SWA_Q_BLOCK = 128
B_IN = 2 * SWA_WIDTH

kernel_name = 'hybrid_nsa_yoco_swa_sink'


def rms_norm(x, g):
    xf = x.astype(jnp.float32)
    y = xf * lax.rsqrt(jnp.mean(xf * xf, axis=-1, keepdims=True) + RMS_EPS)
    return (y * g.astype(jnp.float32)).astype(x.dtype)


def rope_partial(t, pos):
    rot_dim = t.shape[-1] // ROPE_FRACTION
    half = rot_dim // 2
    inv_freq = ROPE_THETA ** (-jnp.arange(half, dtype=jnp.float32) / half)
    ang = pos.astype(jnp.float32)[:, None] * inv_freq[None, :]
    cos, sin = jnp.cos(ang), jnp.sin(ang)
    tr = t[..., :rot_dim].astype(jnp.float32)
    t1, t2 = tr[..., :half], tr[..., half:]
    rot = jnp.concatenate([t1 * cos - t2 * sin, t1 * sin + t2 * cos], axis=-1)
    return jnp.concatenate([rot.astype(t.dtype), t[..., rot_dim:]], axis=-1)


def masked_softmax(s, mask):
    return jax.nn.softmax(jnp.where(mask, s, NEG_BIG), axis=-1)


def split_cols(t, sizes):
    return jnp.split(t, [int(v) for v in np.cumsum(sizes)[:-1]], axis=-1)


def compress_blocks(blocks, pos_emb, w1, w2):
    b, g, n, l, d = blocks.shape
    flat = (blocks + pos_emb).reshape(b, g, n, l * d)
    return jax.nn.silu(flat @ w1) @ w2


def nsa_mixer(hn, pos, w_in, w_out, pos_k, w1_k, w2_k, pos_v, w1_v, w2_v):
    B, S, _ = hn.shape
    G, HPG, DH, H = NSA_KV_GROUPS, NSA_HPG, NSA_HEAD_DIM, NSA_HEADS
    QB, W = NSA_Q_BLOCK, NSA_WINDOW
    scale = 1.0 / math.sqrt(DH)
    q, k_c, v_c, k_s, v_s, k_w, v_w, g_logit, z = split_cols(hn @ w_in, A_SIZES)
    q = q.reshape(B, S, G, HPG, DH).transpose(0, 2, 3, 1, 4)
    q_rot = rope_partial(q, pos)

    def heads(t):
        return t.reshape(B, S, G, DH).transpose(0, 2, 1, 3)

    k_c, v_c, k_s, v_s, k_w, v_w = (heads(t) for t in (k_c, v_c, k_s, v_s, k_w, v_w))
    k_s = rope_partial(k_s, pos)
    k_w = rope_partial(k_w, pos)
    gates = jax.nn.sigmoid(g_logit.astype(jnp.float32)).reshape(B, S, N_BRANCH, H)
    z = z.reshape(B, S, N_BRANCH, H, DH)

    n_cmp = (S - CMP_LEN) // CMP_STRIDE + 1
    cmp_idx = np.arange(n_cmp)[:, None] * CMP_STRIDE + np.arange(CMP_LEN)[None, :]
    k_cmp = compress_blocks(k_c[:, :, cmp_idx], pos_k, w1_k, w2_k)
    v_cmp = compress_blocks(v_c[:, :, cmp_idx], pos_v, w1_v, w2_v)
    cmp_last = jnp.asarray(cmp_idx[:, -1], dtype=jnp.int32)

    n_sel = S // SEL_BLOCK
    n_top = min(SEL_TOPK, n_sel)
    c0 = np.arange(n_cmp) * CMP_STRIDE
    s0 = np.arange(n_sel) * SEL_BLOCK
    cmp_to_sel = jnp.asarray((c0[:, None] < s0[None, :] + SEL_BLOCK) & (c0[:, None] + CMP_LEN > s0[None, :]), dtype=jnp.float32)
    k_blk = k_s.reshape(B, G, n_sel, SEL_BLOCK, DH)
    v_blk = v_s.reshape(B, G, n_sel, SEL_BLOCK, DH)
    blk_ids = jnp.arange(n_sel, dtype=jnp.int32)
    b_ix = jnp.arange(B)[:, None, None, None]
    g_ix = jnp.arange(G)[None, :, None, None]

    k_wpad = jnp.pad(k_w, ((0, 0), (0, 0), (W, 0), (0, 0)))
    v_wpad = jnp.pad(v_w, ((0, 0), (0, 0), (W, 0), (0, 0)))

    def chunk(c):
        t0 = c * QB
        tq = t0 + jnp.arange(QB, dtype=jnp.int32)
        q_c = lax.dynamic_slice_in_dim(q, t0, QB, axis=3)
        q_r = lax.dynamic_slice_in_dim(q_rot, t0, QB, axis=3)
        valid_c = cmp_last[None, :] <= tq[:, None]
        s_c = jnp.einsum('bghtd,bgnd->bghtn', q_c, k_cmp).astype(jnp.float32) * scale
        p_c = jnp.where(valid_c, masked_softmax(s_c, valid_c), 0.0)
        o_c = jnp.einsum('bghtn,bgnd->bghtd', p_c.astype(v_cmp.dtype), v_cmp)
        imp = jnp.einsum('bghtn,nj->bgtj', p_c, cmp_to_sel)
        dist = (tq // SEL_BLOCK)[:, None] - blk_ids[None, :]
        forced = (blk_ids[None, :] == 0) | ((dist >= 0) & (dist < SEL_LOCAL))
        imp = jnp.where(forced, FORCE_SCORE, imp)
        imp = jnp.where(dist >= 0, imp, -1.0)
        _, sel = lax.top_k(imp, n_top)
        k_sel = k_blk[b_ix, g_ix, sel].reshape(B, G, QB, n_top * SEL_BLOCK, DH)
        v_sel = v_blk[b_ix, g_ix, sel].reshape(B, G, QB, n_top * SEL_BLOCK, DH)
        kpos = (sel[..., None] * SEL_BLOCK + jnp.arange(SEL_BLOCK, dtype=jnp.int32)).reshape(B, G, QB, n_top * SEL_BLOCK)
        valid_s = (kpos <= tq[None, None, :, None])[:, :, None]
        s_s = jnp.einsum('bghtd,bgtkd->bghtk', q_r, k_sel).astype(jnp.float32) * scale
        p_s = masked_softmax(s_s, valid_s)
        o_s = jnp.einsum('bghtk,bgtkd->bghtd', p_s.astype(v_sel.dtype), v_sel)
        kp = t0 - W + jnp.arange(QB + W, dtype=jnp.int32)
        k_win = lax.dynamic_slice_in_dim(k_wpad, t0, QB + W, axis=2)
        v_win = lax.dynamic_slice_in_dim(v_wpad, t0, QB + W, axis=2)
        valid_w = (kp[None, :] <= tq[:, None]) & (kp[None, :] > tq[:, None] - W) & (kp[None, :] >= 0)
        s_w = jnp.einsum('bghtd,bgkd->bghtk', q_r, k_win).astype(jnp.float32) * scale
        p_w = masked_softmax(s_w, valid_w)
        o_w = jnp.einsum('bghtk,bgkd->bghtd', p_w.astype(v_win.dtype), v_win)
        branch = jnp.stack([o_c, o_s, o_w], axis=2)
        branch = branch.transpose(0, 4, 2, 1, 3, 5).reshape(B, QB, N_BRANCH, H, DH)
        g_c = lax.dynamic_slice_in_dim(gates, t0, QB, axis=1)
        z_c = lax.dynamic_slice_in_dim(z, t0, QB, axis=1)
        mixed = jnp.sum(g_c[..., None] * branch.astype(jnp.float32) * jax.nn.silu(z_c.astype(jnp.float32)), axis=2)
        return mixed.astype(hn.dtype).reshape(B, QB, NSA_WIDTH)

    o = lax.map(chunk, jnp.arange(S // QB))
    o = o.transpose(1, 0, 2, 3).reshape(B, S, NSA_WIDTH)
    return o @ w_out


def shared_kv(h, pos, kv_norm, w_kv):
    B, S, _ = h.shape
    k, v = split_cols(rms_norm(h, kv_norm) @ w_kv, (SWA_KV_WIDTH, SWA_KV_WIDTH))
    k = k.reshape(B, S, SWA_KV_HEADS, SWA_HEAD_DIM).transpose(0, 2, 1, 3)
    v = v.reshape(B, S, SWA_KV_HEADS, SWA_HEAD_DIM).transpose(0, 2, 1, 3)
    return rope_partial(k, pos), v


def swa_sink_mixer(hn, pos, w_in, w_out, sinks, k_sh, v_sh):
    B, S, _ = hn.shape
    G, HPG, DH, QB = SWA_KV_HEADS, SWA_HPG, SWA_HEAD_DIM, SWA_Q_BLOCK
    scale = 1.0 / math.sqrt(DH)
    q, z = split_cols(hn @ w_in, (SWA_WIDTH, SWA_WIDTH))
    q = rope_partial(q.reshape(B, S, G, HPG, DH).transpose(0, 2, 3, 1, 4), pos)
    k_pad = jnp.pad(k_sh, ((0, 0), (0, 0), (QB, 0), (0, 0)))
    v_pad = jnp.pad(v_sh, ((0, 0), (0, 0), (QB, 0), (0, 0)))
    sink = sinks.astype(jnp.float32).reshape(1, G, HPG, 1, 1)

    def block(c):
        t0 = c * QB
        tq = t0 + jnp.arange(QB, dtype=jnp.int32)
        kp = t0 - QB + jnp.arange(2 * QB, dtype=jnp.int32)
        qb = lax.dynamic_slice_in_dim(q, t0, QB, axis=3)
        kb = lax.dynamic_slice_in_dim(k_pad, t0, 2 * QB, axis=2)
        vb = lax.dynamic_slice_in_dim(v_pad, t0, 2 * QB, axis=2)
        s = jnp.einsum('bghtd,bgkd->bghtk', qb, kb).astype(jnp.float32) * scale
        mask = (kp[None, :] <= tq[:, None]) & (kp[None, :] > tq[:, None] - SWA_WINDOW) & (kp[None, :] >= 0)
        s = jnp.where(mask, s, NEG_BIG)
        m = jnp.maximum(jnp.max(s, axis=-1, keepdims=True), sink)
        e = jnp.exp(s - m)
        prob = e / (jnp.sum(e, axis=-1, keepdims=True) + jnp.exp(sink - m))
        o = jnp.einsum('bghtk,bgkd->bghtd', prob.astype(vb.dtype), vb)
        return o.transpose(0, 3, 1, 2, 4).reshape(B, QB, SWA_WIDTH)

    o = lax.map(block, jnp.arange(S // QB)).transpose(1, 0, 2, 3).reshape(B, S, SWA_WIDTH)
    return (o * jax.nn.silu(z)) @ w_out


def setup_inputs(seed: int = 0) -> dict:
    key = jax.random.key(seed)
    ks = jax.random.split(key, 22)
    f32 = jnp.float32

    def nrm(k, shape, scale):
        return jax.random.normal(k, shape, f32) * scale

    def gain(k, shape):
        return 1.0 + 0.05 * jax.random.normal(k, shape, f32)

    cin = CMP_LEN * NSA_HEAD_DIM
    return {
        'x': nrm(ks[0], (BATCH, SEQ, D_MODEL), 1.0),
        'p': nrm(ks[1], (DEPTH, BATCH, SEQ, PLE_DIM), 1.0),
        'a_norm': gain(ks[2], (N_A_LAYERS, D_MODEL)),
        'a_w_in': nrm(ks[3], (N_A_LAYERS, D_MODEL, A_IN), D_MODEL ** -0.5),
        'a_w_out': nrm(ks[4], (N_A_LAYERS, NSA_WIDTH, D_MODEL), NSA_WIDTH ** -0.5),
        'a_cmp_pos_k': nrm(ks[5], (N_A_LAYERS, CMP_LEN, NSA_HEAD_DIM), 0.1),
        'a_cmp_w1_k': nrm(ks[6], (N_A_LAYERS, cin, CMP_HIDDEN), cin ** -0.5),
        'a_cmp_w2_k': nrm(ks[7], (N_A_LAYERS, CMP_HIDDEN, NSA_HEAD_DIM), CMP_HIDDEN ** -0.5),
        'a_cmp_pos_v': nrm(ks[8], (N_A_LAYERS, CMP_LEN, NSA_HEAD_DIM), 0.1),
        'a_cmp_w1_v': nrm(ks[9], (N_A_LAYERS, cin, CMP_HIDDEN), cin ** -0.5),
        'a_cmp_w2_v': nrm(ks[10], (N_A_LAYERS, CMP_HIDDEN, NSA_HEAD_DIM), CMP_HIDDEN ** -0.5),
        'kv_norm': gain(ks[11], (D_MODEL,)),
        'w_kv': nrm(ks[12], (D_MODEL, 2 * SWA_KV_WIDTH), D_MODEL ** -0.5),
        'b_norm': gain(ks[13], (N_B_LAYERS, D_MODEL)),
        'b_w_in': nrm(ks[14], (N_B_LAYERS, D_MODEL, B_IN), D_MODEL ** -0.5),
        'b_w_out': nrm(ks[15], (N_B_LAYERS, SWA_WIDTH, D_MODEL), SWA_WIDTH ** -0.5),
        'b_sinks': nrm(ks[16], (N_B_LAYERS, SWA_HEADS), 0.5),
        'ple_norm': gain(ks[17], (DEPTH, D_MODEL)),
        'ple_gate_w': nrm(ks[18], (DEPTH, D_MODEL, D_MODEL), D_MODEL ** -0.5),
        'ple_proj': nrm(ks[19], (DEPTH, PLE_DIM, D_MODEL), PLE_DIM ** -0.5),
        'final_norm': gain(ks[20], (D_MODEL,)),
    }


def reference(x, p, a_norm, a_w_in, a_w_out, a_cmp_pos_k, a_cmp_w1_k, a_cmp_w2_k, a_cmp_pos_v, a_cmp_w1_v, a_cmp_w2_v, kv_norm, w_kv, b_norm, b_w_in, b_w_out, b_sinks, ple_norm, ple_gate_w, ple_proj, final_norm):
    S = x.shape[1]
    pos = jnp.arange(S, dtype=jnp.int32)
    h = x
    k_sh = None
    v_sh = None
    for i in range(DEPTH):
        if i < N_A_LAYERS:
            a = i
            h = h + nsa_mixer(rms_norm(h, a_norm[a]), pos, a_w_in[a], a_w_out[a], a_cmp_pos_k[a], a_cmp_w1_k[a], a_cmp_w2_k[a], a_cmp_pos_v[a], a_cmp_w1_v[a], a_cmp_w2_v[a])
        else:
            if i == N_A_LAYERS:
                k_sh, v_sh = shared_kv(h, pos, kv_norm, w_kv)
            b = i - N_A_LAYERS
            h = h + swa_sink_mixer(rms_norm(h, b_norm[b]), pos, b_w_in[b], b_w_out[b], b_sinks[b], k_sh, v_sh)
        gate = jax.nn.sigmoid(rms_norm(h, ple_norm[i]) @ ple_gate_w[i])
        h = h + gate * (p[i] @ ple_proj[i])
    return rms_norm(h, final_norm)
```

```python
import contextlib
import math
import numpy as np
import ml_dtypes
import concourse.bass as bass
import concourse.mybir as mybir
from concourse.bass_utils import run_bass_kernel_spmd

F32 = mybir.dt.float32
BF16 = mybir.dt.bfloat16
AF = mybir.ActivationFunctionType
ALU = mybir.AluOpType
AX = mybir.AxisListType
NPBF = ml_dtypes.bfloat16

ENGS = ["pe", "act", "dve", "pool", "sp"]
SEM_CHUNK = 12000
DMA_CHUNK = 700


class Op:
    __slots__ = ("eng", "idx", "fn", "deps", "flag", "lane", "lane_n", "count", "is_dma")

    def __init__(self, eng, idx, fn, is_dma, lane):
        self.eng = eng
        self.idx = idx
        self.fn = fn
        self.deps = []
        self.flag = False
        self.is_dma = is_dma
        self.lane = lane
        self.lane_n = 0
        self.count = 0


class _Rec:
    def __getattr__(self, name):
        return lambda *a, **k: (name, a, k)


_REC = _Rec()


class Arena:
    def __init__(self, t, nbytes):
        self.t = t
        self.nbytes = nbytes
        self.off = 0

    def reset(self):
        self.off = 0

    def alloc(self, shape, dtype, parts=128):
        esz = 4 if dtype == F32 else 2
        n = int(np.prod(shape[1:]))
        nb = (n * esz + 63) // 64 * 64
        assert self.off + nb <= self.nbytes, f"arena overflow {self.off + nb} > {self.nbytes}"
        v = self.t[0:shape[0], self.off // 2:(self.off + n * esz) // 2]
        self.off += nb
        if dtype == F32:
            v = v.bitcast(F32)
        if len(shape) == 3:
            v = v.rearrange("p (a b) -> p a b", a=shape[1])
        elif len(shape) == 4:
            v = v.rearrange("p (a b c) -> p a b c", a=shape[1], b=shape[2])
        return v


class Prog:
    def __init__(self, nc):
        self.nc = nc
        self.ops = {e: [] for e in ENGS}
        self.last_w = {}
        self.readers = {}
        self.lane_cnt = {}
        self.stack = contextlib.ExitStack()
        self.nsem = 0
        self.rot = {}
        self.bar = []
        self.lane_last = {}

    def sbuf(self, name, shape, dtype):
        return self.stack.enter_context(self.nc.sbuf_tensor(name, list(shape), dtype))

    def psum(self, name, shape, dtype):
        return self.stack.enter_context(self.nc.psum_tensor(name, list(shape), dtype))

    def nxt(self, key, n):
        v = self.rot.get(key, 0)
        self.rot[key] = v + 1
        return v % n

    def op(self, eng, fn, r=(), w=(), lane=None):
        is_dma = lane is not None
        o = Op(eng, len(self.ops[eng]), fn(_REC), is_dma, lane)
        deps = {id(b): b for b in self.bar}
        for res in r:
            lw = self.last_w.get(res)
            if lw is not None:
                deps[id(lw)] = lw
            if isinstance(res, str) and res.startswith("ps") and res[2:].isdigit():
                for rd in self.readers.get(res, ()):
                    if rd.eng != eng:
                        deps[id(rd)] = rd
        for res in w:
            lw = self.last_w.get(res)
            if lw is not None:
                deps[id(lw)] = lw
            for rd in self.readers.get(res, ()):
                deps[id(rd)] = rd
        for d in deps.values():
            if d.eng == "pe" and eng == "pe" and not d.is_dma and not is_dma:
                continue
            o.deps.append(d)
            d.flag = True
        for res in w:
            self.last_w[res] = o
            self.readers[res] = []
        for res in r:
            if res in w:
                continue
            self.readers.setdefault(res, []).append(o)
        if is_dma:
            n = self.lane_cnt.get(lane, 0) + 1
            self.lane_cnt[lane] = n
            o.lane_n = n
            self.lane_last[lane] = o
        self.ops[eng].append(o)
        return o

    def barrier(self):
        b = []
        for e in ENGS:
            for o in reversed(self.ops[e]):
                if not o.is_dma:
                    b.append(o)
                    break
        b.extend(self.lane_last.values())
        for o in b:
            o.flag = True
        self.bar = b

    def emit(self, final_waits=()):
        nc = self.nc
        st = self.stack
        eng_sems = {}
        for e in ENGS:
            c = 0
            for o in self.ops[e]:
                if o.flag and not o.is_dma:
                    c += 1
                    o.count = c
            nch = max((c + SEM_CHUNK - 1) // SEM_CHUNK, 1)
            eng_sems[e] = [st.enter_context(nc.semaphore(f"s_{e}_{i}")) for i in range(nch)]
            self.nsem += nch
        lane_sems = {}
        for lane, n in self.lane_cnt.items():
            nch = (n + DMA_CHUNK - 1) // DMA_CHUNK
            lane_sems[lane] = [st.enter_context(nc.semaphore(f"l_{len(lane_sems)}_{i}")) for i in range(nch)]
            self.nsem += nch
        assert self.nsem < 245, f"too many semaphores {self.nsem}"

        def token(o):
            if o.is_dma:
                ch = (o.lane_n - 1) // DMA_CHUNK
                return lane_sems[o.lane][ch], ((o.lane_n - 1) % DMA_CHUNK + 1) * 16, ("L", o.lane, ch)
            ch = (o.count - 1) // SEM_CHUNK
            return eng_sems[o.eng][ch], (o.count - 1) % SEM_CHUNK + 1, ("E", o.eng, ch)

        block = st.enter_context(nc.Block())
        handles = {"pe": "tensor", "act": "scalar", "dve": "vector", "pool": "gpsimd", "sp": "sync"}
        self.nwaits = 0

        def make(e):
            def body(eng):
                waited = {}
                for o in self.ops[e]:
                    need = {}
                    for d in o.deps:
                        sem, val, key = token(d)
                        if waited.get(key, 0) >= val:
                            continue
                        if key not in need or need[key][1] < val:
                            need[key] = (sem, val)
                    for key, (sem, val) in need.items():
                        eng.wait_ge(sem, val)
                        waited[key] = val
                        self.nwaits += 1
                    name, a, k = o.fn
                    ins = getattr(eng, name)(*a, **k)
                    if o.is_dma:
                        sem, val, _ = token(o)
                        ins.then_inc(sem, 16)
                    elif o.flag:
                        sem, val, _ = token(o)
                        ins.then_inc(sem, 1)
                if e == "sp":
                    for o in final_waits:
                        sem, val, key = token(o)
                        eng.wait_ge(sem, val)
            return body

        for e in ENGS:
            getattr(block, handles[e])(make(e))
        st.close()


D = 4096
S = 4096
NTILE = 32
Q0 = 15
NQ = 17
NQ1 = 16
A_IN = 19552
SC_A = 1.0 / math.sqrt(128.0)
SC_B = 1.0 / math.sqrt(64.0)


def build(stop=99, dbg=False):
    nc = bass.Bass("TRN2", target_bir_lowering=False)
    P = Prog(nc)

    def din(name, shape, dt):
        return nc.dram_tensor(name, list(shape), dt, kind="ExternalInput").ap()

    def dscr(name, shape, dt):
        kind = "ExternalOutput" if (dbg and name in dbg) else "Internal"
        return nc.dram_tensor(name, list(shape), dt, kind=kind).ap()

    xbuf = din("xbuf", [NTILE, 128, D], F32)
    pbuf = din("pbuf", [2, NQ, 128, 256], F32)
    a_w_in = din("a_w_in", [D, A_IN], F32)
    a_w_out = din("a_w_out", [D, D], F32)
    w1k = din("w1k", [4096, 256], F32)
    w1v = din("w1v", [4096, 256], F32)
    w2k = din("w2k", [256, 128], F32)
    w2v = din("w2v", [256, 128], F32)
    posTk = din("posTk", [128, 32], F32)
    posTv = din("posTv", [128, 32], F32)
    w_kv = din("w_kv", [D, 1024], F32)
    b_w_in = din("b_w_in", [D, 8192], F32)
    b_w_out = din("b_w_out", [D, D], F32)
    gate_w = din("gate_w", [2, D, D], F32)
    ple_proj = din("ple_proj", [2, 256, D], F32)
    gains = din("gains", [6, D], F32)
    sinks = din("sinks", [64], F32)
    c_ident = din("c_ident", [128, 128], BF16)
    c_mle = din("c_mle", [128, 128], BF16)
    c_mgt = din("c_mgt", [128, 128], BF16)
    c_emat = din("c_emat", [64, 4096], BF16)
    c_m2s = din("c_m2s", [128, 2, 64], BF16)
    c_kval = din("c_kval", [128, NTILE, 8], BF16)
    c_cosA = din("c_cosA", [128, NTILE, 16], F32)
    c_sinA = din("c_sinA", [128, NTILE, 16], F32)
    c_cosB = din("c_cosB", [128, NTILE, 8], F32)
    c_sinB = din("c_sinB", [128, NTILE, 8], F32)
    c_keep = din("c_keep", [128, NQ, 64], F32)
    c_add = din("c_add", [128, NQ, 64], F32)
    c_cmask = din("c_cmask", [128, NQ, 2, 128], BF16)
    out = nc.dram_tensor("out", [NQ1, 128, D], F32, kind="ExternalOutput").ap()

    QTu = dscr("QTu", [NQ, 128, 32, 128], BF16)
    QTr = dscr("QTr", [NQ, 128, 32, 128], BF16)
    KTs = dscr("KTs", [4, 128, S], BF16)
    KTw = dscr("KTw", [4, 128, S], BF16)
    KcT = dscr("KcT", [4, 128, S], BF16)
    VcT = dscr("VcT", [4, 128, S], BF16)
    Vs = dscr("Vs", [NTILE, 128, 4, 129], BF16)
    Vw = dscr("Vw", [NTILE, 128, 4, 129], BF16)
    GT = dscr("GT", [NQ, 128, 96], F32)
    ZS = dscr("ZS", [NQ, 128, 12288], F32)
    MIXT = dscr("MIXT", [NQ, 128, 32, 128], BF16)
    H1 = dscr("H1", [NQ, 128, D], F32)
    H2 = dscr("H2", [NQ, 128, D], F32)
    KT1 = dscr("KT1", [8, 64, NQ * 128], BF16)
    V1 = dscr("V1", [NQ, 128, 8, 65], BF16)
    QT1 = dscr("QT1", [NQ1, 8, 64, 8, 128], BF16)
    Z1 = dscr("Z1", [NQ1, 128, D], F32)
    MIXT1 = dscr("MIXT1", [NQ1, 128, 32, 128], BF16)
    H3 = dscr("H3", [NQ1, 128, D], F32)
    H4 = dscr("H4", [NQ1, 128, D], F32)

    idt = P.sbuf("idt", [128, 128], BF16)
    mle = P.sbuf("mle", [128, 128], BF16)
    mgt = P.sbuf("mgt", [128, 128], BF16)
    kval = P.sbuf("kval", [128, NTILE, 8], BF16)
    cosA = P.sbuf("cosA", [128, NTILE, 16], F32)
    sinA = P.sbuf("sinA", [128, NTILE, 16], F32)
    cosB = P.sbuf("cosB", [128, NTILE, 8], F32)
    sinB = P.sbuf("sinB", [128, NTILE, 8], F32)
    kcmpT = P.sbuf("kcmpT", [128, 4, 256], BF16)
    vcmp = P.sbuf("vcmp", [128, 4, 2, 129], BF16)
    ss = P.sbuf("ss", [128, 8], F32)
    ssq = [P.sbuf(f"ssq{i}", [128, 8], F32) for i in range(2)]
    rstd = [P.sbuf(f"rstd{i}", [128, 8], F32) for i in range(2)]
    rtmp2 = P.sbuf("rtmp2", [128, 16], F32)
    sm = P.sbuf("sm", [128, 64], F32)
    for t_, s_, nm in ((idt, c_ident, "idt"), (mle, c_mle, "mle"), (mgt, c_mgt, "mgt"), (kval, c_kval, "kval"),
                       (cosA, c_cosA, "cosA"), (sinA, c_sinA, "sinA"), (cosB, c_cosB, "cosB"), (sinB, c_sinB, "sinB")):
        P.op("sp", lambda e: e.dma_start(out=t_[:], in_=s_), w=[nm], lane=nm)
    ARENA_BYTES = 176 * 1024
    AR = Arena(P.sbuf("arena", [128, ARENA_BYTES // 2], BF16), ARENA_BYTES)
    ps = [P.psum(f"ps{i}", [128, 512], F32) for i in range(8)]
    psb = [p[:].bitcast(BF16) for p in ps]

    NTB = 5
    dn = {}

    def dense_layout():
        AR.reset()
        dn["xb"] = [AR.alloc([128, D], F32) for _ in range(2)]
        dn["gbc"] = AR.alloc([128, D], F32)
        dn["xs"] = [AR.alloc([128, D], BF16) for _ in range(2)]
        dn["hnT"] = AR.alloc([128, 32, NTB * 128], BF16)
        dn["wb"] = [AR.alloc([128, 8, 512], BF16) for _ in range(5)]
        dn["ub"] = [AR.alloc([128, 512], BF16) for _ in range(2)]
        dn["rb"] = [AR.alloc([128, 512], BF16) for _ in range(2)]
        dn["rtmp"] = [AR.alloc([128, 128], F32) for _ in range(4)]
        dn["tst"] = [AR.alloc([128, 8, 128], BF16) for _ in range(2)]
        dn["vst"] = [AR.alloc([128, 4, 129], BF16) for _ in range(2)]
        dn["v1st"] = [AR.alloc([128, 8, 65], BF16) for _ in range(2)]
        dn["fst"] = [AR.alloc([128, 512], F32) for _ in range(2)]
        dn["pT"] = AR.alloc([128, 2, NTB * 128], BF16)
        dn["pws"] = [AR.alloc([128, 2, 512], BF16) for _ in range(2)]
        dn["pxb"] = AR.alloc([128, 256], F32)
        dn["pxh"] = AR.alloc([128, 256], BF16)
        dn["rsb"] = [AR.alloc([128, 512], F32) for _ in range(2)]

    def load_gain(gi):
        P.op("sp", lambda e: e.dma_start(out=dn["gbc"], in_=gains[gi].partition_broadcast(128)), w=["gbc"], lane="gbc")

    def norm_stage(src, deps, par, i):
        s = P.nxt("xb", 2)
        xb_, xs, gbc = dn["xb"][s], dn["xs"][s], dn["gbc"]
        P.op("sp", lambda e: e.dma_start(out=xb_, in_=src), r=list(deps), w=[f"xb{s}"], lane=f"xb{s}")
        P.op("act", lambda e: e.activation(out=xs, in_=xb_, func=AF.Square, accum_out=ssq[par][:, i:i + 1]), r=[f"xb{s}"], w=[f"xs{s}", ("ssq", par, i)])
        return s

    def norm_stage2(s, i):
        xb_, xs, gbc = dn["xb"][s], dn["xs"][s], dn["gbc"]
        P.op("dve", lambda e: e.tensor_tensor(out=xs, in0=xb_, in1=gbc, op=ALU.mult), r=[f"xb{s}", "gbc"], w=[f"xs{s}"])
        transpose_rows(xs, f"xs{s}", i)

    def rstd_cols(par, lo, hi):
        P.op("dve", lambda e: e.tensor_scalar(out=rtmp2[:, lo:hi], in0=ssq[par][:, lo:hi], scalar1=1.0 / D, scalar2=1e-6, op0=ALU.mult, op1=ALU.add),
             r=[("ssq", par, i) for i in range(lo, hi)], w=[("rtmp2", lo)])
        P.op("act", lambda e: e.activation(out=rtmp2[:, 8 + lo:8 + hi], in_=rtmp2[:, lo:hi], func=AF.Sqrt), r=[("rtmp2", lo)], w=[("rtmp2b", lo)])
        P.op("dve", lambda e: e.reciprocal(out=rstd[par][:, lo:hi], in_=rtmp2[:, 8 + lo:8 + hi]), r=[("rtmp2b", lo)], w=[("rstd", par)])

    cur = {"par": 0, "scale": False}

    def rs(i):
        return rstd[cur["par"]][:, i:i + 1] if cur["scale"] else 1.0

    def rsdep():
        return [("rstd", cur["par"])] if cur["scale"] else []

    class NormPipe:
        def __init__(self, blocks, src_of, per_tile=None):
            self.blocks = blocks
            self.src_of = src_of
            self.per_tile = per_tile
            self.par0 = P.nxt("normpar", 2)
            self.seq = [(bi, i) for bi, blk in enumerate(blocks) for i in range(len(blk))]
            self.idx = {k: n for n, k in enumerate(self.seq)}
            self.slot = {}
            self.nA = 0
            self.nS = 0
            self._fill()

        def par(self, bi):
            return (self.par0 + bi) % 2

        def ga(self, bi):
            n = len(self.blocks[bi])
            return 2 if n >= 4 else (1 if n >= 2 else n)

        def _fill(self):
            while self.nA < len(self.seq) and self.nA < self.nS + 2:
                bi, i = self.seq[self.nA]
                src, deps = self.src_of(self.blocks[bi][i])
                self.slot[self.nA] = norm_stage(src, deps, self.par(bi), i)
                self.nA += 1

        def _upto(self, bi, i):
            tgt = self.idx[(bi, i)]
            while self.nS <= tgt:
                b2, i2 = self.seq[self.nS]
                norm_stage2(self.slot[self.nS], i2)
                if self.per_tile is not None:
                    self.per_tile(i2, self.blocks[b2][i2])
                self.nS += 1
                self._fill()

        def start(self, bi):
            g = self.ga(bi)
            self._upto(bi, g - 1)
            rstd_cols(self.par(bi), 0, g)
            cur["par"] = self.par(bi)
            cur["scale"] = True

        def after_first(self, bi):
            n, g = len(self.blocks[bi]), self.ga(bi)
            if n > g:
                self._upto(bi, n - 1)
                rstd_cols(self.par(bi), g, n)

        def after_last(self, bi):
            if bi + 1 < len(self.blocks):
                self._upto(bi + 1, self.ga(bi + 1) - 1)

    def transpose_rows(srcbf, sname, i):
        hnT = dn["hnT"]
        for q in range(4):
            bank = (0, 6)[P.nxt("tb", 2)]
            for j in range(8):
                kc = q * 8 + j
                P.op("pe", lambda e: e.transpose(out=psb[bank][:, j * 128:(j + 1) * 128], in_=srcbf[:, kc * 128:(kc + 1) * 128], identity=idt[:]),
                     r=[sname, "idt"], w=[f"ps{bank}"])
            src = psb[bank].rearrange("p (j c) -> p j c", j=8)
            dst = hnT[:, q * 8:(q + 1) * 8, i * 128:(i + 1) * 128]
            if q % 2 == 0:
                P.op("act", lambda e: e.copy(out=dst, in_=src), r=[f"ps{bank}"], w=[f"hnT{i}"])
            else:
                P.op("dve", lambda e: e.tensor_copy(out=dst, in_=src), r=[f"ps{bank}"], w=[f"hnT{i}"])

    def dense(ntl, W, col_blocks, epilogue, hook_first=None, hook_last=None):
        hnT, wb = dn["hnT"], dn["wb"]
        wv = W.rearrange("(kc p) c -> p kc c", p=128)
        for cbi, (c0, n) in enumerate(col_blocks):
            slots = []
            for part in range(4):
                s = P.nxt("wb", 5)
                slots.append(s)
                P.op("pool", lambda e: e.dma_start(out=wb[s][:, :, 0:n], in_=wv[:, part * 8:(part + 1) * 8, c0:c0 + n]), w=[f"wb{s}"], lane=f"wb{s}")
            ga = 2 if ntl >= 4 else (1 if ntl >= 2 else ntl)
            for gidx, (grp, banks) in enumerate(((list(range(0, ga)), (1, 2)), (list(range(ga, ntl)), (3, 4, 5)))):
                if not grp:
                    continue
                for part in range(4):
                    s = slots[part]
                    for gi, i in enumerate(grp):
                        bank = banks[gi]
                        for k8 in range(8):
                            kc = part * 8 + k8
                            P.op("pe", lambda e: e.matmul(ps[bank][:, 0:n], lhsT=hnT[:, kc, i * 128:(i + 1) * 128], rhs=wb[s][:, k8, 0:n], start=(kc == 0), stop=(kc == 31)),
                                 r=[f"hnT{i}", f"wb{s}"], w=[f"ps{bank}"])
                for gi, i in enumerate(grp):
                    epilogue(cbi, i, banks[gi])
                if gidx == 0:
                    if cbi == 0 and hook_first is not None:
                        hook_first()
                    if cbi == len(col_blocks) - 1 and hook_last is not None:
                        hook_last()

    def epi_T(bank, tile, nh, hd, rope, want_u, want_r, store, i):
        half = hd // 8
        ub, rb, rtmp, tst = dn["ub"], dn["rb"], dn["rtmp"], dn["tst"]
        psv = ps[bank][:].rearrange("p (h d) -> p h d", h=nh)
        srcs = []
        if want_u:
            u = P.nxt("ub", 2)
            P.op("act", lambda e: e.activation(out=ub[u], in_=ps[bank][:], func=AF.Copy, scale=rs(i)), r=[f"ps{bank}"] + rsdep(), w=[f"ub{u}"])
            srcs.append((ub[u], f"ub{u}"))
        if want_r:
            r_ = P.nxt("rb", 2)
            P.op("act", lambda e: e.activation(out=rb[r_], in_=ps[bank][:], func=AF.Copy, scale=rs(i)), r=[f"ps{bank}"] + rsdep(), w=[f"rb{r_}"])
            cs, sn = (cosA, sinA) if rope == "A" else (cosB, sinB)
            cname, sname = ("cosA", "sinA") if rope == "A" else ("cosB", "sinB")
            cb_ = cs[:, tile:tile + 1, :].broadcast_to([128, nh, half])
            sb_ = sn[:, tile:tile + 1, :].broadcast_to([128, nh, half])
            t1 = psv[:, :, 0:half]
            t2 = psv[:, :, half:2 * half]
            tm = [rtmp[k][:, 0:nh * half].rearrange("p (h d) -> p h d", h=nh) for k in range(4)]
            P.op("dve", lambda e: e.scalar_tensor_tensor(out=tm[0], in0=t1, scalar=rs(i), in1=cb_, op0=ALU.mult, op1=ALU.mult), r=[f"ps{bank}", cname] + rsdep(), w=["rt0"])
            P.op("dve", lambda e: e.scalar_tensor_tensor(out=tm[1], in0=t2, scalar=rs(i), in1=sb_, op0=ALU.mult, op1=ALU.mult), r=[f"ps{bank}", sname] + rsdep(), w=["rt1"])
            P.op("dve", lambda e: e.scalar_tensor_tensor(out=tm[2], in0=t1, scalar=rs(i), in1=sb_, op0=ALU.mult, op1=ALU.mult), r=[f"ps{bank}", sname] + rsdep(), w=["rt2"])
            P.op("dve", lambda e: e.scalar_tensor_tensor(out=tm[3], in0=t2, scalar=rs(i), in1=cb_, op0=ALU.mult, op1=ALU.mult), r=[f"ps{bank}", cname] + rsdep(), w=["rt3"])
            rbv = rb[r_].rearrange("p (h d) -> p h d", h=nh)
            P.op("dve", lambda e: e.tensor_tensor(out=rbv[:, :, 0:half], in0=tm[0], in1=tm[1], op=ALU.subtract), r=["rt0", "rt1"], w=[f"rb{r_}"])
            P.op("dve", lambda e: e.tensor_tensor(out=rbv[:, :, half:2 * half], in0=tm[2], in1=tm[3], op=ALU.add), r=["rt2", "rt3"], w=[f"rb{r_}"])
            srcs.append((rb[r_], f"rb{r_}"))
        tb = (7, 0, 6)[P.nxt("eb", 3)]
        slot = 0
        for (sb, sname_) in srcs:
            for h in range(nh):
                P.op("pe", lambda e: e.transpose(out=psb[tb][0:hd, slot * 128:(slot + 1) * 128], in_=sb[:, h * hd:(h + 1) * hd], identity=idt[:]),
                     r=[sname_, "idt"], w=[f"ps{tb}"])
                slot += 1
        st_ = P.nxt("tst", 2)
        nsl = slot
        P.op("dve", lambda e: e.tensor_copy(out=tst[st_][0:hd, 0:nsl, :], in_=psb[tb][0:hd, 0:nsl * 128].rearrange("p (j c) -> p j c", j=nsl)),
             r=[f"ps{tb}"], w=[f"tst{st_}"])
        store(tst[st_], f"tst{st_}")

    def epi_V(bank, tile, dst, key, i):
        vst = dn["vst"]
        v = P.nxt("vst", 2)
        P.op("act", lambda e: e.activation(out=vst[v][:, :, 0:128], in_=ps[bank][:].rearrange("p (h d) -> p h d", h=4), func=AF.Copy, scale=rs(i)), r=[f"ps{bank}"] + rsdep(), w=[f"vst{v}"])
        P.op("dve", lambda e: e.tensor_copy(out=vst[v][:, :, 128:129], in_=kval[:, tile, 0:4].unsqueeze(2)), r=["kval", f"vst{v}"], w=[f"vst{v}"])
        P.op("sp", lambda e: e.dma_start(out=dst, in_=vst[v]), r=[f"vst{v}"], w=[key], lane=f"vst{v}")

    def epi_F(bank, func, n, dst, key, i):
        fst = dn["fst"]
        f = P.nxt("fst", 2)
        P.op("act", lambda e: e.activation(out=fst[f][:, 0:n], in_=ps[bank][:, 0:n], func=func, scale=rs(i)), r=[f"ps{bank}"] + rsdep(), w=[f"fst{f}"])
        P.op("sp", lambda e: e.dma_start(out=dst, in_=fst[f][:, 0:n]), r=[f"fst{f}"], w=[key], lane=f"fst{f}")

    def finish_early():
        fw = [P.last_w[k] for k in list(P.last_w) if isinstance(k, tuple)]
        for o in fw:
            o.flag = True
        P.emit(final_waits=fw)
        return nc, P

    dense_layout()
    load_gain(0)
    KV_BLOCKS = [(4096 + 512 * k, 512) for k in range(6)]
    Q_BLOCKS = [(512 * k, 512) for k in range(8)]
    G_BLOCK = [(7168, 96)]
    Z_BLOCKS = [(7264 + 512 * k, 512) for k in range(24)]

    def p1_block(t0, ntl, own):
        bi_ = p1_index[t0]
        p1_pipe.start(bi_)
        blocks = (Q_BLOCKS if own else []) + KV_BLOCKS + ((G_BLOCK + Z_BLOCKS) if own else [])
        nq = 8 if own else 0

        def epi(cbi, i, bank):
            tile = t0 + i
            lt = tile - Q0
            if cbi < nq:
                def store(stg, key):
                    P.op("sp", lambda e: e.dma_start(out=QTu[lt][:, 4 * cbi:4 * cbi + 4, :], in_=stg[:, 0:4, :]), r=[key], w=[("QTu", lt, cbi)], lane=key + "a")
                    P.op("sp", lambda e: e.dma_start(out=QTr[lt][:, 4 * cbi:4 * cbi + 4, :], in_=stg[:, 4:8, :]), r=[key], w=[("QTr", lt, cbi)], lane=key + "b")
                epi_T(bank, tile, 4, 128, "A", True, True, store, i)
            elif cbi < nq + 6:
                k = cbi - nq
                if k in (0, 1, 2, 4):
                    dstT, nm = {0: (KcT, "KcT"), 1: (VcT, "VcT"), 2: (KTs, "KTs"), 4: (KTw, "KTw")}[k]

                    def store(stg, key):
                        P.op("sp", lambda e: e.dma_start(out=dstT[:, :, tile * 128:(tile + 1) * 128].rearrange("g p t -> p g t"), in_=stg[:, 0:4, :]),
                             r=[key], w=[(nm, tile)], lane=key + "a")
                    if k < 2:
                        epi_T(bank, tile, 4, 128, None, True, False, store, i)
                    else:
                        epi_T(bank, tile, 4, 128, "A", False, True, store, i)
                else:
                    dstV = Vs if k == 3 else Vw
                    epi_V(bank, tile, dstV[tile], ("Vs" if k == 3 else "Vw", tile), i)
            elif cbi == nq + 6:
                epi_F(bank, AF.Sigmoid, 96, GT[lt], ("GT", lt), i)
            else:
                zc = cbi - nq - 7
                epi_F(bank, AF.Silu, 512, ZS[lt][:, zc * 512:(zc + 1) * 512], ("ZS", lt, zc), i)

        dense(ntl, a_w_in, blocks, epi, hook_first=lambda: p1_pipe.after_first(bi_), hook_last=lambda: p1_pipe.after_last(bi_))

    P1_BLOCKS = [(0, 5, False), (5, 5, False), (10, 5, False), (15, 5, True), (20, 4, True), (24, 4, True), (28, 4, True)]
    p1_index = {t0: bi for bi, (t0, _, _) in enumerate(P1_BLOCKS)}
    p1_pipe = NormPipe([list(range(t0, t0 + n_)) for (t0, n_, _) in P1_BLOCKS], lambda tile: (xbuf[tile], ()))
    for (t0, ntl, own) in P1_BLOCKS:
        p1_block(t0, ntl, own)
    if stop < 2:
        return finish_early()

    P.barrier()
    AR.reset()
    w1b = AR.alloc([128, 32, 256], BF16)
    w2b = AR.alloc([128, 2, 128], BF16)
    posb = AR.alloc([128, 32], BF16)
    hb = AR.alloc([128, 2], F32)
    hT = AR.alloc([128, 2, 256], BF16)
    kct = AR.alloc([128, 256, 16], BF16)
    kctd = AR.alloc([128, 16, 256], BF16)
    P.op("dve", lambda e: e.memset(hT, 0.0), w=["hT"])
    P.op("dve", lambda e: e.memset(vcmp[:], 1.0), w=["vcmp"])
    for kv in range(2):
        w1, w2, posT, srcT, nm = (w1k, w2k, posTk, KcT, "KcT") if kv == 0 else (w1v, w2v, posTv, VcT, "VcT")
        P.op("pool", lambda e: e.dma_start(out=w1b, in_=w1.rearrange("(l p) c -> p l c", p=128)), w=["w1b"], lane="w1b")
        P.op("pool", lambda e: e.dma_start(out=w2b, in_=w2.rearrange("(c p) d -> p c d", p=128)), w=["w2b"], lane="w2b")
        P.op("pool", lambda e: e.dma_start(out=posb, in_=posT), w=["posb"], lane="posb")
        for hc in range(2):
            for l in range(32):
                P.op("pe", lambda e: e.matmul(ps[0][:, hc:hc + 1], lhsT=w1b[:, l, hc * 128:(hc + 1) * 128], rhs=posb[:, l:l + 1], start=(l == 0), stop=(l == 31)),
                     r=["w1b", "posb"], w=["ps0"])
            P.op("dve", lambda e: e.tensor_copy(out=hb[:, hc:hc + 1], in_=ps[0][:, hc:hc + 1]), r=["ps0"], w=["hb"])
        for g in range(4):
            P.op("sp", lambda e: e.dma_start(out=kct.rearrange("p n s -> p (n s)"), in_=srcT[g]), r=[(nm, t) for t in range(NTILE)], w=["kct"], lane="kct")
            P.op("dve", lambda e: e.tensor_copy(out=kctd, in_=kct.rearrange("p n s -> p s n")), r=["kct"], w=["kctd"])
            for hc in range(2):
                bank = 2 + hc
                for l in range(32):
                    P.op("pe", lambda e: e.matmul(ps[bank][:, 0:255], lhsT=w1b[:, l, hc * 128:(hc + 1) * 128], rhs=kctd[:, l % 16, l // 16:l // 16 + 255], start=(l == 0), stop=(l == 31)),
                         r=["w1b", "kctd"], w=[f"ps{bank}"])
                P.op("act", lambda e: e.activation(out=hT[:, hc, 0:255], in_=ps[bank][:, 0:255], func=AF.Silu, bias=hb[:, hc:hc + 1]),
                     r=[f"ps{bank}", "hb"], w=["hT"])
            if kv == 0:
                for hc in range(2):
                    P.op("pe", lambda e: e.matmul(ps[4][:, 0:256], lhsT=w2b[:, hc, :], rhs=hT[:, hc, :], start=(hc == 0), stop=(hc == 1)), r=["w2b", "hT"], w=["ps4"])
                P.op("dve", lambda e: e.tensor_copy(out=kcmpT[:, g, :], in_=ps[4][:, 0:256]), r=["ps4"], w=["kcmpT"])
            else:
                for nt_ in range(2):
                    for hc in range(2):
                        P.op("pe", lambda e: e.matmul(ps[4 + nt_][:, 0:128], lhsT=hT[:, hc, nt_ * 128:(nt_ + 1) * 128], rhs=w2b[:, hc, :], start=(hc == 0), stop=(hc == 1)),
                             r=["w2b", "hT"], w=[f"ps{4 + nt_}"])
                    P.op("dve", lambda e: e.tensor_copy(out=vcmp[:, g, nt_, 0:128], in_=ps[4 + nt_][:, 0:128]), r=[f"ps{4 + nt_}"], w=["vcmp"])
    if stop < 3:
        return finish_early()

    P.barrier()
    AR.reset()
    emat = AR.alloc([64, 4096], BF16)
    m2s = AR.alloc([128, 2, 64], BF16)
    keep = AR.alloc([128, NQ, 64], F32)
    addm = AR.alloc([128, NQ, 64], F32)
    cmask = AR.alloc([128, NQ, 2, 128], BF16)
    for t_, s_, nm in ((emat, c_emat, "emat"), (m2s, c_m2s, "m2s"), (keep, c_keep, "keep"), (addm, c_add, "addm"), (cmask, c_cmask, "cmask")):
        P.op("sp", lambda e: e.dma_start(out=t_, in_=s_), w=[nm], lane=nm)
    kts = AR.alloc([128, S], BF16)
    ktw = AR.alloc([128, S], BF16)
    vsb = AR.alloc([128, NTILE, 129], BF16)
    vwb = AR.alloc([128, NTILE, 129], BF16)
    qu = [AR.alloc([128, 8, 128], BF16) for _ in range(2)]
    qr = [AR.alloc([128, 8, 128], BF16) for _ in range(2)]
    gt = [AR.alloc([128, 96], F32) for _ in range(2)]
    zsb = [AR.alloc([128, 3, 1024], F32) for _ in range(2)]
    pbh = [AR.alloc([128, 4, 128], BF16) for _ in range(6)]
    smk = [AR.alloc([128, 128], BF16) for _ in range(3)]
    selT = AR.alloc([64, 128], BF16)
    selm = AR.alloc([128, 64], BF16)
    impt = AR.alloc([128, 8, 64], F32)
    imp = AR.alloc([128, 4, 64], F32)
    m8 = AR.alloc([128, 16], F32)
    mixa2 = [AR.alloc([128, 8, 128], F32) for _ in range(2)]
    mixb = AR.alloc([128, 8, 128], F32)
    osb2 = [AR.alloc([128, 3, 387], F32) for _ in range(2)]
    mixh = AR.alloc([128, 1024], BF16)
    tst3 = [AR.alloc([128, 8, 128], BF16) for _ in range(2)]
    OB = (4, 5, 6)

    def ov(h):
        return ps[OB[h // 3]][:, (h % 3) * 129:(h % 3) * 129 + 129]

    def att_s1(kT, kname, qsrc, qname, hf, mask, scale):
        sb = P.nxt("sbk", 4)
        P.op("pe", lambda e: e.matmul(ps[sb][:], lhsT=kT, rhs=qsrc[:, 4 * hf:4 * hf + 4, :], start=True, stop=True),
             r=[kname, qname], w=[f"ps{sb}"])
        p_ = P.nxt("pbh", 6)
        P.op("act", lambda e: e.activation(out=pbh[p_], in_=ps[sb][:].rearrange("p (h t) -> p h t", h=4), func=AF.Exp, scale=scale),
             r=[f"ps{sb}"], w=[f"pbh{p_}"])
        if mask is not None:
            mk, mname = mask
            P.op("dve", lambda e: e.tensor_tensor(out=pbh[p_], in0=pbh[p_], in1=mk.unsqueeze(1).broadcast_to([128, 4, 128]), op=ALU.mult),
                 r=[f"pbh{p_}", mname], w=[f"pbh{p_}"])
        return p_

    def att_s2(p_, hf, vtile, vname, first, extra=None):
        for hh in range(4):
            h = 4 * hf + hh
            st_flag = first and (h % 3 == 0)
            P.op("pe", lambda e: e.matmul(ov(h), lhsT=pbh[p_][:, hh, :], rhs=vtile, start=st_flag, stop=False, skip_group_check=True),
                 r=[f"pbh{p_}", vname], w=[f"ps{OB[h // 3]}"])
        if extra is not None:
            extra(p_, hf)

    def run_branch(items, look=2, hooks=None):
        pend = []
        for n_, it in enumerate(items):
            mask = it["mask"]() if callable(it["mask"]) else it["mask"]
            for hf in range(2):
                p_ = att_s1(it["kT"], it["kname"], it["q"], it["qname"], hf, mask, SC_A)
                pend.append((p_, hf, it["v"], it["vname"], n_ == 0, it.get("extra")))
                if len(pend) > look:
                    att_s2(*pend.pop(0))
            if hooks and n_ in hooks:
                hooks[n_]()
        while pend:
            att_s2(*pend.pop(0))

    def fin1():
        ob = P.nxt("osb", 2)
        osb = osb2[ob]
        for bi in range(3):
            nh_ = 3 if bi < 2 else 2
            P.op("act", lambda e: e.copy(out=osb[:, bi, 0:nh_ * 129], in_=ps[OB[bi]][:, 0:nh_ * 129]), r=[f"ps{OB[bi]}"], w=[("osb", ob, bi)])
        for bi in range(3):
            nh_ = 3 if bi < 2 else 2
            P.op("dve", lambda e: e.tensor_scalar(out=sm[:, 32 * ob + 3 * bi:32 * ob + 3 * bi + nh_], in0=osb[:, bi, 0:nh_ * 129].rearrange("p (h d) -> p h d", d=129)[:, :, 128],
                                                   scalar1=1e-30, scalar2=None, op0=ALU.max),
                 r=[("osb", ob, bi)], w=[("smrs", ob, bi)])
        P.op("dve", lambda e: e.reciprocal(out=sm[:, 32 * ob + 8:32 * ob + 16], in_=sm[:, 32 * ob:32 * ob + 8]), r=[("smrs", ob, 0), ("smrs", ob, 1), ("smrs", ob, 2)], w=[("sm_ri", ob)])
        return ob

    def fin2(ob, br, g, gtt, gname, zt, zname, first_branch, mi_):
        osb = osb2[ob]
        mixa = mixa2[mi_]
        man = f"mixa{mi_}"
        P.op("dve", lambda e: e.tensor_tensor(out=sm[:, 32 * ob + 16:32 * ob + 24], in0=sm[:, 32 * ob + 8:32 * ob + 16], in1=gtt[:, br * 32 + g * 8:br * 32 + g * 8 + 8], op=ALU.mult),
             r=[("sm_ri", ob), gname], w=[("sm_f", ob)])
        dst = mixa if first_branch else mixb
        dname = man if first_branch else "mixb"
        for bi in range(3):
            nh_ = 3 if bi < 2 else 2
            P.op("dve", lambda e: e.tensor_tensor(out=dst[:, 3 * bi:3 * bi + nh_, :], in0=osb[:, bi, 0:nh_ * 129].rearrange("p (h d) -> p h d", d=129)[:, :, 0:128],
                                                   in1=sm[:, 32 * ob + 16 + 3 * bi:32 * ob + 16 + 3 * bi + nh_].unsqueeze(2).broadcast_to([128, nh_, 128]), op=ALU.mult),
                 r=[("osb", ob, bi), ("sm_f", ob)], w=[(dname, bi)])
        zv = zt[:, br, :].rearrange("p (h d) -> p h d", h=8)
        allb = [(dname, 0), (dname, 1), (dname, 2)]
        P.op("dve", lambda e: e.tensor_tensor(out=dst, in0=dst, in1=zv, op=ALU.mult), r=allb + [zname], w=allb)
        if not first_branch:
            alla = [(man, 0), (man, 1), (man, 2)]
            P.op("dve", lambda e: e.tensor_tensor(out=mixa, in0=mixa, in1=mixb, op=ALU.add), r=alla + allb, w=alla)

    def mix_tail(n):
        g, qt = iters[n]
        mi_ = n % 2
        man = f"mixa{mi_}"
        P.op("act", lambda e: e.copy(out=mixh, in_=mixa2[mi_].rearrange("p h d -> p (h d)")), r=[(man, 0), (man, 1), (man, 2)], w=["mixh"])
        for h in range(8):
            P.op("pe", lambda e: e.transpose(out=psb[7][:, h * 128:(h + 1) * 128], in_=mixh[:, h * 128:(h + 1) * 128], identity=idt[:]), r=["mixh", "idt"], w=["ps7"])
        st_ = P.nxt("tst3", 2)
        P.op("dve", lambda e: e.tensor_copy(out=tst3[st_], in_=psb[7][:].rearrange("p (j c) -> p j c", j=8)), r=["ps7"], w=[f"tst3{st_}"])
        P.op("sp", lambda e: e.dma_start(out=MIXT[qt][:, g * 8:(g + 1) * 8, :], in_=tst3[st_]), r=[f"tst3{st_}"], w=[("MIXT", qt, g)], lane=f"tst3{st_}")

    iters = [(g, qt) for g in range(4) for qt in range(NQ)]
    slot_of = {}

    def att_loads(n):
        g, qt = iters[n]
        qs = P.nxt("qs", 2)
        slot_of[n] = qs
        P.op("sp", lambda e: e.dma_start(out=qu[qs], in_=QTu[qt][:, g * 8:(g + 1) * 8, :]), r=[("QTu", qt, 2 * g), ("QTu", qt, 2 * g + 1)], w=[f"qu{qs}"], lane=f"qu{qs}")
        P.op("sp", lambda e: e.dma_start(out=qr[qs], in_=QTr[qt][:, g * 8:(g + 1) * 8, :]), r=[("QTr", qt, 2 * g), ("QTr", qt, 2 * g + 1)], w=[f"qr{qs}"], lane=f"qr{qs}")
        P.op("sp", lambda e: e.dma_start(out=gt[qs], in_=GT[qt]), r=[("GT", qt)], w=[f"gt{qs}"], lane=f"gt{qs}")
        P.op("sp", lambda e: e.dma_start(out=zsb[qs], in_=ZS[qt].rearrange("p (b c) -> p b c", b=3)[:, :, g * 1024:(g + 1) * 1024]),
             r=[("ZS", qt, zc) for zc in range(24)], w=[f"zsb{qs}"], lane=f"zsb{qs}")

    att_loads(0)
    for n, (g, qt) in enumerate(iters):
        if qt == 0:
            P.op("sp", lambda e: e.dma_start(out=kts, in_=KTs[g]), r=[("KTs", t) for t in range(NTILE)], w=["kts"], lane="kts")
            P.op("sp", lambda e: e.dma_start(out=ktw, in_=KTw[g]), r=[("KTw", t) for t in range(NTILE)], w=["ktw"], lane="ktw")
            P.op("sp", lambda e: e.dma_start(out=vsb, in_=Vs[:, :, g, :].rearrange("n p d -> p n d")), r=[("Vs", t) for t in range(NTILE)], w=["vsb"], lane="vsb")
            P.op("sp", lambda e: e.dma_start(out=vwb, in_=Vw[:, :, g, :].rearrange("n p d -> p n d")), r=[("Vw", t) for t in range(NTILE)], w=["vwb"], lane="vwb")
        if n + 1 < len(iters):
            att_loads(n + 1)
        if True:
            j = Q0 + qt
            qs = slot_of[n]
            fargs = (g, gt[qs], f"gt{qs}", zsb[qs], f"zsb{qs}")

            def imp_mm(p_, hf, nt_):
                for hh in range(4):
                    h = 4 * hf + hh
                    P.op("pe", lambda e: e.matmul(ps[7][:, h * 64:(h + 1) * 64], lhsT=pbh[p_][:, hh, :], rhs=m2s[:, nt_, :], start=(nt_ == 0 and h == 0), stop=False, skip_group_check=True),
                         r=[f"pbh{p_}", "m2s"], w=["ps7"])
            items = []
            for nt_ in range(2):
                items.append(dict(kT=kcmpT[:, g, nt_ * 128:(nt_ + 1) * 128], kname="kcmpT", q=qu[qs], qname=f"qu{qs}", mask=(cmask[:, qt, nt_, :], "cmask"),
                                  v=vcmp[:, g, nt_, :], vname="vcmp", extra=(lambda p_, hf, nt_=nt_: imp_mm(p_, hf, nt_))))
            run_branch(items)

            def selection_dve(ob):
                P.op("dve", lambda e: e.tensor_tensor(out=impt, in0=ps[7][:].rearrange("p (h j) -> p h j", h=8), in1=sm[:, 32 * ob + 8:32 * ob + 16].unsqueeze(2).broadcast_to([128, 8, 64]), op=ALU.mult),
                     r=["ps7", ("sm_ri", ob)], w=["impt"])
                P.op("dve", lambda e: e.tensor_reduce(out=imp[:, 0, :], in_=impt.rearrange("p h j -> p j h"), axis=AX.X, op=ALU.add), r=["impt"], w=["imp0"])
                P.op("dve", lambda e: e.tensor_tensor(out=imp[:, 0, :], in0=imp[:, 0, :], in1=keep[:, qt, :], op=ALU.mult), r=["imp0", "keep"], w=["imp0"])
                P.op("dve", lambda e: e.tensor_tensor(out=imp[:, 0, :], in0=imp[:, 0, :], in1=addm[:, qt, :], op=ALU.add), r=["imp0", "addm"], w=["imp0"])
                P.op("dve", lambda e: e.max(out=m8[:, 0:8], in_=imp[:, 0, :]), r=["imp0"], w=["m8a"])
                P.op("dve", lambda e: e.match_replace(out=imp[:, 1, :], in_to_replace=m8[:, 0:8], in_values=imp[:, 0, :], imm_value=-2.0), r=["imp0", "m8a"], w=["imp1"])
                P.op("dve", lambda e: e.max(out=m8[:, 8:16], in_=imp[:, 1, :]), r=["imp1"], w=["m8b"])
                P.op("dve", lambda e: e.match_replace(out=imp[:, 2, :], in_to_replace=m8[:, 8:16], in_values=imp[:, 1, :], imm_value=-2.0), r=["imp1", "m8b"], w=["imp2"])
                P.op("dve", lambda e: e.tensor_tensor(out=selm, in0=imp[:, 0, :], in1=imp[:, 2, :], op=ALU.not_equal), r=["imp0", "imp2"], w=["selm"])

            ob_c = fin1()
            items = []
            kts_w = [kt for kt in range(j - 4, j + 1) if kt >= 0]
            for n_, kt in enumerate(kts_w):
                mask = (mle[:], "mle") if kt == j else ((mgt[:], "mgt") if kt == j - 4 else None)
                items.append(dict(kT=ktw[:, kt * 128:(kt + 1) * 128], kname="ktw", q=qr[qs], qname=f"qr{qs}", mask=mask, v=vwb[:, kt, :], vname="vwb"))
            run_branch(items, hooks={0: (lambda ob_c=ob_c: selection_dve(ob_c))})
            fin2(ob_c, 0, *fargs, True, n % 2)
            P.op("pe", lambda e: e.transpose(out=psb[7][0:64, 0:128], in_=selm, identity=idt[:]), r=["selm", "idt"], w=["ps7"])
            P.op("dve", lambda e: e.tensor_copy(out=selT, in_=psb[7][0:64, 0:128]), r=["ps7"], w=["selT"])
            ob_w = fin1()

            def mk_pre(kt):
                def pre():
                    P.op("pe", lambda e: e.matmul(ps[7][:, 0:128], lhsT=emat[:, kt * 128:(kt + 1) * 128], rhs=selT, start=True, stop=True), r=["emat", "selT"], w=["ps7"])
                    mi = P.nxt("smk", 3)
                    if kt == j:
                        P.op("dve", lambda e: e.tensor_tensor(out=smk[mi], in0=ps[7][:, 0:128], in1=mle[:], op=ALU.mult), r=["ps7", "mle"], w=[f"smk{mi}"])
                    else:
                        P.op("dve", lambda e: e.tensor_copy(out=smk[mi], in_=ps[7][:, 0:128]), r=["ps7"], w=[f"smk{mi}"])
                    return (smk[mi], f"smk{mi}")
                return pre
            items = []
            for kt in range(j + 1):
                items.append(dict(kT=kts[:, kt * 128:(kt + 1) * 128], kname="kts", q=qr[qs], qname=f"qr{qs}", mask=mk_pre(kt), v=vsb[:, kt, :], vname="vsb"))

            def hook_sel(n=n, ob_w=ob_w, fargs=fargs):
                fin2(ob_w, 2, *fargs, False, n % 2)
                if n > 0:
                    mix_tail(n - 1)
            run_branch(items, hooks={1: hook_sel})
            ob_s = fin1()
            fin2(ob_s, 1, *fargs, False, n % 2)
    mix_tail(len(iters) - 1)
    if stop < 4:
        return finish_early()

    CB8 = [(512 * k, 512) for k in range(8)]

    def outproj_pass(tiles, MT, mtname, nmt, W, Hsrc_fn, Hdst, hname):
        hnT, rsb = dn["hnT"], dn["rsb"]
        cur["scale"] = False
        for i, (lt, _) in enumerate(tiles):
            P.op("sp", lambda e: e.dma_start(out=hnT[:, :, i * 128:(i + 1) * 128], in_=MT[lt]), r=[(mtname, lt, g) for g in range(nmt)], w=[f"hnT{i}"], lane=f"hnT{i}")

        def epi(cbi, i, bank):
            lt, src = tiles[i]
            fst = dn["fst"]
            f = P.nxt("fst", 2)
            P.op("act", lambda e: e.copy(out=fst[f], in_=ps[bank][:]), r=[f"ps{bank}"], w=[f"fst{f}"])
            r_ = P.nxt("rsb", 2)
            P.op("sp", lambda e: e.dma_start(out=rsb[r_], in_=src[:, cbi * 512:(cbi + 1) * 512]), r=Hsrc_fn(lt, cbi), w=[f"rsb{r_}"], lane=f"rsb{r_}")
            P.op("dve", lambda e: e.tensor_tensor(out=rsb[r_], in0=fst[f], in1=rsb[r_], op=ALU.add), r=[f"fst{f}", f"rsb{r_}"], w=[f"rsb{r_}"])
            P.op("sp", lambda e: e.dma_start(out=Hdst[lt][:, cbi * 512:(cbi + 1) * 512], in_=rsb[r_]), r=[f"rsb{r_}"], w=[(hname, lt, cbi)], lane=f"rsb{r_}s")
        dense(len(tiles), W, CB8, epi)

    def ple_pass(layer, pipe, bi, tiles, Hsrc, hsname, Hdst, hdname, pidx_fn):
        pT, pws, pxb, pxh, rsb, fst = dn["pT"], dn["pws"], dn["pxb"], dn["pxh"], dn["rsb"], dn["fst"]
        def per_tile(i, lt):
            P.op("sp", lambda e: e.dma_start(out=pxb, in_=pbuf[layer, pidx_fn(lt)]), w=["pxb"], lane="pxb")
            P.op("act", lambda e: e.copy(out=pxh, in_=pxb), r=["pxb"], w=["pxh"])
            for c in range(2):
                P.op("pe", lambda e: e.transpose(out=psb[7][:, c * 128:(c + 1) * 128], in_=pxh[:, c * 128:(c + 1) * 128], identity=idt[:]), r=["pxh", "idt"], w=["ps7"])
            P.op("dve", lambda e: e.tensor_copy(out=pT[:, :, i * 128:(i + 1) * 128], in_=psb[7][:, 0:256].rearrange("p (j c) -> p j c", j=2)), r=["ps7"], w=[f"pT{i}"])
        pipe.per_tile = per_tile
        pipe.start(bi)
        pwv = ple_proj[layer].rearrange("(c p) d -> p c d", p=128)
        cur = {}

        def epi(cbi, i, bank):
            lt = tiles[i]
            if i == 0:
                w_ = P.nxt("pws", 2)
                cur["w"] = w_
                P.op("pool", lambda e: e.dma_start(out=pws[w_], in_=pwv[:, :, cbi * 512:(cbi + 1) * 512]), w=[f"pws{w_}"], lane=f"pws{w_}")
            w_ = cur["w"]
            pbk = (7, 0, 6)[P.nxt("eb", 3)]
            for c in range(2):
                P.op("pe", lambda e: e.matmul(ps[pbk][:], lhsT=pT[:, c, i * 128:(i + 1) * 128], rhs=pws[w_][:, c, :], start=(c == 0), stop=(c == 1)),
                     r=[f"pT{i}", f"pws{w_}"], w=[f"ps{pbk}"])
            f = P.nxt("fst", 2)
            P.op("act", lambda e: e.activation(out=fst[f], in_=ps[bank][:], func=AF.Sigmoid, scale=rs(i)), r=[f"ps{bank}"] + rsdep(), w=[f"fst{f}"])
            P.op("dve", lambda e: e.tensor_tensor(out=fst[f], in0=ps[pbk][:], in1=fst[f], op=ALU.mult), r=[f"ps{pbk}", f"fst{f}"], w=[f"fst{f}"])
            r_ = P.nxt("rsb", 2)
            P.op("sp", lambda e: e.dma_start(out=rsb[r_], in_=Hsrc[lt][:, cbi * 512:(cbi + 1) * 512]), r=[(hsname, lt, cbi)], w=[f"rsb{r_}"], lane=f"rsb{r_}")
            P.op("dve", lambda e: e.tensor_tensor(out=rsb[r_], in0=rsb[r_], in1=fst[f], op=ALU.add), r=[f"rsb{r_}", f"fst{f}"], w=[f"rsb{r_}"])
            P.op("sp", lambda e: e.dma_start(out=Hdst[lt][:, cbi * 512:(cbi + 1) * 512], in_=rsb[r_]), r=[f"rsb{r_}"], w=[(hdname, lt, cbi)], lane=f"rsb{r_}s")
        dense(len(tiles), gate_w[layer], CB8, epi, hook_first=lambda: pipe.after_first(bi), hook_last=lambda: pipe.after_last(bi))

    def rsb_f32_stage(w_):
        return dn["pwf"][w_]

    L0_BLOCKS = [[0, 1, 2, 3, 4], [5, 6, 7, 8], [9, 10, 11, 12], [13, 14, 15, 16]]
    L1_BLOCKS = [[0, 1, 2, 3], [4, 5, 6, 7], [8, 9, 10, 11], [12, 13, 14, 15]]

    P.barrier()
    dense_layout()
    for blk in L0_BLOCKS:
        outproj_pass([(lt, xbuf[Q0 + lt]) for lt in blk], MIXT, "MIXT", 4, a_w_out, lambda lt, c: [], H1, "H1")
    load_gain(1)
    pipe = NormPipe(L0_BLOCKS, lambda lt: (H1[lt], [("H1", lt, c) for c in range(8)]))
    for bi, blk in enumerate(L0_BLOCKS):
        ple_pass(0, pipe, bi, blk, H1, "H1", H2, "H2", lambda lt: lt)
    if stop < 5:
        return finish_early()

    load_gain(2)
    pipe = NormPipe(L0_BLOCKS, lambda lt: (H2[lt], [("H2", lt, c) for c in range(8)]))
    for bi, blk in enumerate(L0_BLOCKS):
        pipe.start(bi)

        def epi(cbi, i, bank):
            lt = blk[i]
            tile = Q0 + lt
            if cbi == 0:
                def store(stg, key):
                    P.op("sp", lambda e: e.dma_start(out=KT1[:, :, lt * 128:(lt + 1) * 128].rearrange("g p t -> p g t"), in_=stg[0:64, 0:8, :]), r=[key], w=[("KT1", lt)], lane=key + "a")
                epi_T(bank, tile, 8, 64, "B", False, True, store, i)
            else:
                v1st = dn["v1st"]
                v = P.nxt("v1st", 2)
                P.op("act", lambda e: e.activation(out=v1st[v][:, :, 0:64], in_=ps[bank][:].rearrange("p (h d) -> p h d", h=8), func=AF.Copy, scale=rs(i)), r=[f"ps{bank}"] + rsdep(), w=[f"v1st{v}"])
                P.op("dve", lambda e: e.tensor_copy(out=v1st[v][:, :, 64:65], in_=kval[:, tile, :].unsqueeze(2)), r=["kval", f"v1st{v}"], w=[f"v1st{v}"])
                P.op("sp", lambda e: e.dma_start(out=V1[lt], in_=v1st[v]), r=[f"v1st{v}"], w=[("V1", lt)], lane=f"v1st{v}")
        dense(len(blk), w_kv, [(0, 512), (512, 512)], epi, hook_first=lambda bi=bi, pipe=pipe: pipe.after_first(bi), hook_last=lambda bi=bi, pipe=pipe: pipe.after_last(bi))

    load_gain(3)
    pipe = NormPipe(L1_BLOCKS, lambda l1: (H2[l1 + 1], [("H2", l1 + 1, c) for c in range(8)]))
    for bi, blk in enumerate(L1_BLOCKS):
        pipe.start(bi)

        def epi(cbi, i, bank):
            l1 = blk[i]
            tile = 16 + l1
            if cbi < 8:
                def store(stg, key):
                    P.op("sp", lambda e: e.dma_start(out=QT1[l1, cbi], in_=stg[0:64, 0:8, :]), r=[key], w=[("QT1", l1, cbi)], lane=key + "a")
                epi_T(bank, tile, 8, 64, "B", False, True, store, i)
            else:
                zc = cbi - 8
                epi_F(bank, AF.Silu, 512, Z1[l1][:, zc * 512:(zc + 1) * 512], ("Z1", l1, zc), i)
        dense(len(blk), b_w_in, [(512 * k, 512) for k in range(16)], epi, hook_first=lambda bi=bi, pipe=pipe: pipe.after_first(bi), hook_last=lambda bi=bi, pipe=pipe: pipe.after_last(bi))
    if stop < 6:
        return finish_early()

    P.barrier()
    AR.reset()
    kt1 = AR.alloc([64, 8, NQ * 128], BF16)
    v1b = AR.alloc([128, NQ, 8 * 65], BF16)
    esk = AR.alloc([128, 64], F32)
    q1 = [AR.alloc([64, 8, 128], BF16) for _ in range(2)]
    z1b = [AR.alloc([128, D], F32) for _ in range(2)]
    mix1 = AR.alloc([128, D], F32)
    mix1h = AR.alloc([128, D], BF16)
    pbh = [AR.alloc([128, 4, 128], BF16) for _ in range(6)]
    tst3 = [AR.alloc([128, 8, 128], BF16) for _ in range(2)]
    o1b = [AR.alloc([128, 2, 260], F32) for _ in range(2)]
    P.op("sp", lambda e: e.dma_start(out=kt1, in_=KT1.rearrange("g p t -> p g t")), r=[("KT1", lt) for lt in range(NQ)], w=["kt1"], lane="kt1")
    P.op("sp", lambda e: e.dma_start(out=v1b, in_=V1.rearrange("n p g d -> p n (g d)")), r=[("V1", lt) for lt in range(NQ)], w=["v1b"], lane="v1b")
    P.op("sp", lambda e: e.dma_start(out=esk, in_=sinks.partition_broadcast(128)), w=["esk"], lane="esk")
    P.op("act", lambda e: e.activation(out=esk, in_=esk, func=AF.Exp), r=["esk"], w=["esk"])

    def ov1(h):
        return ps[4 + h // 4][:, (h % 4) * 65:(h % 4) * 65 + 65]

    def swa_s1(g, qs, lt, hf, mask):
        sb = P.nxt("sbk", 4)
        P.op("pe", lambda e: e.matmul(ps[sb][:], lhsT=kt1[:, g, lt * 128:(lt + 1) * 128], rhs=q1[qs][:, 4 * hf:4 * hf + 4, :], start=True, stop=True),
             r=["kt1", f"q1{qs}"], w=[f"ps{sb}"])
        p_ = P.nxt("pbh", 6)
        P.op("act", lambda e: e.activation(out=pbh[p_], in_=ps[sb][:].rearrange("p (h t) -> p h t", h=4), func=AF.Exp, scale=SC_B),
             r=[f"ps{sb}"], w=[f"pbh{p_}"])
        mk, mname = mask
        P.op("dve", lambda e: e.tensor_tensor(out=pbh[p_], in0=pbh[p_], in1=mk.unsqueeze(1).broadcast_to([128, 4, 128]), op=ALU.mult), r=[f"pbh{p_}", mname], w=[f"pbh{p_}"])
        return p_

    for l1 in range(NQ1):
        zq = P.nxt("z1b", 2)
        P.op("sp", lambda e: e.dma_start(out=z1b[zq], in_=Z1[l1]), r=[("Z1", l1, c) for c in range(8)], w=[f"z1b{zq}"], lane=f"z1b{zq}")
        for g in range(8):
            qs = P.nxt("q1", 2)
            P.op("sp", lambda e: e.dma_start(out=q1[qs], in_=QT1[l1, g]), r=[("QT1", l1, g)], w=[f"q1{qs}"], lane=f"q1{qs}")
            steps = []
            for n_, (lt, mask) in enumerate(((l1, (mgt[:], "mgt")), (l1 + 1, (mle[:], "mle")))):
                for hf in range(2):
                    steps.append((swa_s1(g, qs, lt, hf, mask), hf, lt, n_))
            for (p_, hf, lt, n_) in steps:
                for hh in range(4):
                    h = 4 * hf + hh
                    P.op("pe", lambda e: e.matmul(ov1(h), lhsT=pbh[p_][:, hh, :], rhs=v1b[:, lt, g * 65:(g + 1) * 65], start=(n_ == 0 and h % 4 == 0), stop=False, skip_group_check=True),
                         r=[f"pbh{p_}", "v1b"], w=[f"ps{4 + h // 4}"])
            ob = P.nxt("o1b", 2)
            for bi in range(2):
                P.op("act", lambda e: e.copy(out=o1b[ob][:, bi, :], in_=ps[4 + bi][:, 0:260]), r=[f"ps{4 + bi}"], w=[("o1b", ob, bi)])
            for bi in range(2):
                P.op("dve", lambda e: e.tensor_tensor(out=sm[:, 32 * ob + 4 * bi:32 * ob + 4 * bi + 4], in0=o1b[ob][:, bi, :].rearrange("p (h d) -> p h d", d=65)[:, :, 64], in1=esk[:, g * 8 + 4 * bi:g * 8 + 4 * bi + 4], op=ALU.add),
                     r=[("o1b", ob, bi), "esk"], w=[("smrs", ob, bi)])
            P.op("dve", lambda e: e.reciprocal(out=sm[:, 32 * ob + 8:32 * ob + 16], in_=sm[:, 32 * ob:32 * ob + 8]), r=[("smrs", ob, 0), ("smrs", ob, 1)], w=[("sm_ri", ob)])
            for bi in range(2):
                P.op("dve", lambda e: e.tensor_tensor(out=mix1[:, g * 512 + bi * 256:g * 512 + bi * 256 + 256].rearrange("p (h d) -> p h d", h=4),
                                                       in0=o1b[ob][:, bi, :].rearrange("p (h d) -> p h d", d=65)[:, :, 0:64],
                                                       in1=sm[:, 32 * ob + 8 + 4 * bi:32 * ob + 8 + 4 * bi + 4].unsqueeze(2).broadcast_to([128, 4, 64]), op=ALU.mult),
                     r=[("o1b", ob, bi), ("sm_ri", ob)], w=[("mix1", g, bi)])
            P.op("pool", lambda e: e.tensor_tensor(out=mix1h[:, g * 512:(g + 1) * 512], in0=mix1[:, g * 512:(g + 1) * 512], in1=z1b[zq][:, g * 512:(g + 1) * 512], op=ALU.mult),
                 r=[("mix1", g, 0), ("mix1", g, 1), f"z1b{zq}"], w=[("mix1h", g)])
        for q in range(4):
            for jj in range(8):
                kc = q * 8 + jj
                P.op("pe", lambda e: e.transpose(out=psb[7][:, jj * 128:(jj + 1) * 128], in_=mix1h[:, kc * 128:(kc + 1) * 128], identity=idt[:]),
                     r=[("mix1h", kc // 4), "idt"], w=["ps7"])
            st_ = P.nxt("tst3", 2)
            P.op("dve", lambda e: e.tensor_copy(out=tst3[st_], in_=psb[7][:].rearrange("p (j c) -> p j c", j=8)), r=["ps7"], w=[f"tst3{st_}"])
            P.op("sp", lambda e: e.dma_start(out=MIXT1[l1][:, q * 8:(q + 1) * 8, :], in_=tst3[st_]), r=[f"tst3{st_}"], w=[("MIXT1", l1, q)], lane=f"tst3{st_}")
    if stop < 7:
        return finish_early()

    P.barrier()
    dense_layout()
    for blk in L1_BLOCKS:
        outproj_pass([(l1, H2[l1 + 1]) for l1 in blk], MIXT1, "MIXT1", 4, b_w_out, lambda l1, c: [("H2", l1 + 1, c)], H3, "H3")
    load_gain(4)
    pipe = NormPipe(L1_BLOCKS, lambda l1: (H3[l1], [("H3", l1, c) for c in range(8)]))
    for bi, blk in enumerate(L1_BLOCKS):
        ple_pass(1, pipe, bi, blk, H3, "H3", H4, "H4", lambda l1: l1 + 1)
    load_gain(5)
    fin = []
    for l1 in range(NQ1):
        s = P.nxt("xb", 2)
        xb_, xs, gbc = dn["xb"][s], dn["xs"][s], dn["gbc"]
        P.op("sp", lambda e: e.dma_start(out=xb_, in_=H4[l1]), r=[("H4", l1, c) for c in range(8)], w=[f"xb{s}"], lane=f"xb{s}")
        P.op("act", lambda e: e.activation(out=xs, in_=xb_, func=AF.Square, accum_out=ss[:, 4 * s:4 * s + 1]), r=[f"xb{s}"], w=[f"xs{s}", f"ss0{s}"])
        P.op("dve", lambda e: e.tensor_scalar(out=ss[:, 4 * s + 1:4 * s + 2], in0=ss[:, 4 * s:4 * s + 1], scalar1=1.0 / D, scalar2=1e-6, op0=ALU.mult, op1=ALU.add), r=[f"ss0{s}"], w=[f"ss1{s}"])
        P.op("act", lambda e: e.activation(out=ss[:, 4 * s + 3:4 * s + 4], in_=ss[:, 4 * s + 1:4 * s + 2], func=AF.Sqrt), r=[f"ss1{s}"], w=[f"ss3{s}"])
        P.op("dve", lambda e: e.reciprocal(out=ss[:, 4 * s + 2:4 * s + 3], in_=ss[:, 4 * s + 3:4 * s + 4]), r=[f"ss3{s}"], w=[f"ss2{s}"])
        P.op("dve", lambda e: e.scalar_tensor_tensor(out=xb_, in0=xb_, scalar=ss[:, 4 * s + 2:4 * s + 3], in1=gbc, op0=ALU.mult, op1=ALU.mult),
             r=[f"xb{s}", f"ss2{s}", "gbc"], w=[f"xb{s}"])
        fin.append(P.op("sp", lambda e: e.dma_start(out=out[l1], in_=xb_), r=[f"xb{s}"], w=[("out", l1)], lane=f"xb{s}s"))
    P.emit(final_waits=fin)
    return nc, P


def _consts(half):
    off = 0 if half == 1 else 2048
    tb = np.arange(S)
    ttrue = tb - off
    valid = (ttrue >= 0).astype(np.float32)
    c = {}
    c["c_ident"] = np.eye(128, dtype=np.float32).astype(NPBF)
    k = np.arange(128)[:, None]
    t = np.arange(128)[None, :]
    c["c_mle"] = (k <= t).astype(np.float32).astype(NPBF)
    c["c_mgt"] = (k > t).astype(np.float32).astype(NPBF)
    em = np.zeros((64, S), np.float32)
    em[tb // 64, tb] = 1.0
    c["c_emat"] = em.astype(NPBF)
    n = np.arange(256)
    jj = np.arange(64)
    m = ((16 * n[:, None] < 64 * jj[None, :] + 64) & (16 * n[:, None] + 32 > 64 * jj[None, :])).astype(np.float32)
    m[255] = 0
    c["c_m2s"] = np.ascontiguousarray(m.reshape(2, 128, 64).transpose(1, 0, 2)).astype(NPBF)
    kv = valid.reshape(NTILE, 128).T
    c["c_kval"] = np.ascontiguousarray(np.repeat(kv[:, :, None], 8, axis=2)).astype(NPBF)

    def rope_tab(half_dim):
        inv = (500000.0 ** (-np.arange(half_dim, dtype=np.float32) / half_dim)).astype(np.float32)
        ang = ttrue.astype(np.float32)[:, None] * inv[None, :]
        cs = np.cos(ang).astype(np.float32).reshape(NTILE, 128, half_dim).transpose(1, 0, 2)
        sn = np.sin(ang).astype(np.float32).reshape(NTILE, 128, half_dim).transpose(1, 0, 2)
        return np.ascontiguousarray(cs), np.ascontiguousarray(sn)
    c["c_cosA"], c["c_sinA"] = rope_tab(16)
    c["c_cosB"], c["c_sinB"] = rope_tab(8)
    keep = np.zeros((128, NQ, 64), np.float32)
    add = np.zeros((128, NQ, 64), np.float32)
    cm = np.zeros((128, NQ, 2, 128), np.float32)
    j0 = off // 64
    for qt in range(NQ):
        tq = (Q0 + qt) * 128 + np.arange(128)
        dist = (tq // 64)[:, None] - jj[None, :]
        causal = (dist >= 0) & (jj[None, :] >= j0)
        forced = (jj[None, :] == j0) | ((dist >= 0) & (dist < 2))
        kp = causal & ~forced
        ad = np.where(jj[None, :] == j0, 1e9, 0.0) + np.where(dist == 1, 2e9, 0.0) + np.where(dist == 0, 4e9, 0.0)
        ad = np.where(causal, np.where(forced, ad, 0.0), -1.0)
        keep[:, qt, :] = kp.astype(np.float32)
        add[:, qt, :] = ad
        nvalid = (16 * n[:, None] + 31 <= tq[None, :]) & (16 * n[:, None] >= off) & (n[:, None] < 255)
        cm[:, qt, :, :] = nvalid.astype(np.float32).reshape(2, 128, 128).transpose(1, 0, 2)
    c["c_keep"] = keep
    c["c_add"] = add
    c["c_cmask"] = cm.astype(NPBF)
    return c


_CACHE = {}


def _prep(inputs):
    f = lambda a: np.ascontiguousarray(np.asarray(a, dtype=np.float32))
    x = f(inputs["x"])
    p = f(inputs["p"])
    shared = {
        "a_w_in": f(inputs["a_w_in"])[0], "a_w_out": f(inputs["a_w_out"])[0],
        "w1k": f(inputs["a_cmp_w1_k"])[0], "w1v": f(inputs["a_cmp_w1_v"])[0],
        "w2k": f(inputs["a_cmp_w2_k"])[0], "w2v": f(inputs["a_cmp_w2_v"])[0],
        "posTk": np.ascontiguousarray(f(inputs["a_cmp_pos_k"])[0].T), "posTv": np.ascontiguousarray(f(inputs["a_cmp_pos_v"])[0].T),
        "w_kv": f(inputs["w_kv"]), "b_w_in": f(inputs["b_w_in"])[0], "b_w_out": f(inputs["b_w_out"])[0],
        "gate_w": f(inputs["ple_gate_w"]), "ple_proj": f(inputs["ple_proj"]),
        "gains": np.ascontiguousarray(np.stack([f(inputs["a_norm"])[0], f(inputs["ple_norm"])[0], f(inputs["kv_norm"]), f(inputs["b_norm"])[0],
                                                f(inputs["ple_norm"])[1], f(inputs["final_norm"])], 0)),
        "sinks": f(inputs["b_sinks"])[0],
    }
    consts = [_consts(0), _consts(1)]
    in_maps = []
    for c in range(8):
        b, half = c // 2, c % 2
        if half == 1:
            xb_ = x[b]
            pb_ = p[:, b, Q0 * 128:, :]
        else:
            xb_ = np.concatenate([np.zeros((2048, D), np.float32), x[b, :2048]], 0)
            pb_ = np.concatenate([np.zeros((2, 128, 256), np.float32), p[:, b, :2048, :]], 1)
        m = dict(shared)
        m.update(consts[half])
        m["xbuf"] = np.ascontiguousarray(xb_.reshape(NTILE, 128, D))
        m["pbuf"] = np.ascontiguousarray(pb_.reshape(2, NQ, 128, 256))
        in_maps.append(m)
    return in_maps


def kernel(**inputs):
    in_maps = _prep(inputs)
    if "nc" not in _CACHE:
        _CACHE["nc"] = build()[0]
    res = run_bass_kernel_spmd(_CACHE["nc"], in_maps, core_ids=list(range(8)))
    outp = np.zeros((4, S, D), np.float32)
    for c in range(8):
        b, half = c // 2, c % 2
        o = np.asarray(res.results[c]["out"]).reshape(2048, D)
        outp[b, half * 2048:(half + 1) * 2048] = o
    return outp
```

```python
import contextlib
import math
import numpy as np
import ml_dtypes
import concourse.bass as bass
import concourse.mybir as mybir
from concourse.bass_utils import run_bass_kernel_spmd

F32 = mybir.dt.float32
BF16 = mybir.dt.bfloat16
AF = mybir.ActivationFunctionType
ALU = mybir.AluOpType
AX = mybir.AxisListType
NPBF = ml_dtypes.bfloat16

ENGS = ["pe", "act", "dve", "pool", "sp"]
SEM_CHUNK = 12000
DMA_CHUNK = 700


class Op:
    __slots__ = ("eng", "idx", "fn", "deps", "flag", "lane", "lane_n", "count", "is_dma")

    def __init__(self, eng, idx, fn, is_dma, lane):
        self.eng = eng
        self.idx = idx
        self.fn = fn
        self.deps = []
        self.flag = False
        self.is_dma = is_dma
        self.lane = lane
        self.lane_n = 0
        self.count = 0


class _Rec:
    def __getattr__(self, name):
        return lambda *a, **k: (name, a, k)


_REC = _Rec()


class Arena:
    def __init__(self, t, nbytes):
        self.t = t
        self.nbytes = nbytes
        self.off = 0

    def reset(self):
        self.off = 0

    def alloc(self, shape, dtype, parts=128):
        esz = 4 if dtype == F32 else 2
        n = int(np.prod(shape[1:]))
        nb = (n * esz + 63) // 64 * 64
        assert self.off + nb <= self.nbytes, f"arena overflow {self.off + nb} > {self.nbytes}"
        v = self.t[0:shape[0], self.off // 2:(self.off + n * esz) // 2]
        self.off += nb
        if dtype == F32:
            v = v.bitcast(F32)
        if len(shape) == 3:
            v = v.rearrange("p (a b) -> p a b", a=shape[1])
        elif len(shape) == 4:
            v = v.rearrange("p (a b c) -> p a b c", a=shape[1], b=shape[2])
        return v


class Prog:
    def __init__(self, nc):
        self.nc = nc
        self.ops = {e: [] for e in ENGS}
        self.last_w = {}
        self.readers = {}
        self.lane_cnt = {}
        self.stack = contextlib.ExitStack()
        self.nsem = 0
        self.rot = {}
        self.bar = []
        self.lane_last = {}

    def sbuf(self, name, shape, dtype):
        return self.stack.enter_context(self.nc.sbuf_tensor(name, list(shape), dtype))

    def psum(self, name, shape, dtype):
        return self.stack.enter_context(self.nc.psum_tensor(name, list(shape), dtype))

    def nxt(self, key, n):
        v = self.rot.get(key, 0)
        self.rot[key] = v + 1
        return v % n

    def op(self, eng, fn, r=(), w=(), lane=None):
        is_dma = lane is not None
        o = Op(eng, len(self.ops[eng]), fn(_REC), is_dma, lane)
        deps = {id(b): b for b in self.bar}
        for res in r:
            lw = self.last_w.get(res)
            if lw is not None:
                deps[id(lw)] = lw
            if isinstance(res, str) and res.startswith("ps") and res[2:].isdigit():
                for rd in self.readers.get(res, ()):
                    if rd.eng != eng:
                        deps[id(rd)] = rd
        for res in w:
            lw = self.last_w.get(res)
            if lw is not None:
                deps[id(lw)] = lw
            for rd in self.readers.get(res, ()):
                deps[id(rd)] = rd
        for d in deps.values():
            if d.eng == "pe" and eng == "pe" and not d.is_dma and not is_dma:
                continue
            o.deps.append(d)
            d.flag = True
        for res in w:
            self.last_w[res] = o
            self.readers[res] = []
        for res in r:
            if res in w:
                continue
            self.readers.setdefault(res, []).append(o)
        if is_dma:
            n = self.lane_cnt.get(lane, 0) + 1
            self.lane_cnt[lane] = n
            o.lane_n = n
            self.lane_last[lane] = o
        self.ops[eng].append(o)
        return o

    def barrier(self):
        b = []
        for e in ENGS:
            for o in reversed(self.ops[e]):
                if not o.is_dma:
                    b.append(o)
                    break
        b.extend(self.lane_last.values())
        for o in b:
            o.flag = True
        self.bar = b

    def emit(self, final_waits=()):
        nc = self.nc
        st = self.stack
        eng_sems = {}
        for e in ENGS:
            c = 0
            for o in self.ops[e]:
                if o.flag and not o.is_dma:
                    c += 1
                    o.count = c
            nch = max((c + SEM_CHUNK - 1) // SEM_CHUNK, 1)
            eng_sems[e] = [st.enter_context(nc.semaphore(f"s_{e}_{i}")) for i in range(nch)]
            self.nsem += nch
        lane_sems = {}
        for lane, n in self.lane_cnt.items():
            nch = (n + DMA_CHUNK - 1) // DMA_CHUNK
            lane_sems[lane] = [st.enter_context(nc.semaphore(f"l_{len(lane_sems)}_{i}")) for i in range(nch)]
            self.nsem += nch
        assert self.nsem < 245, f"too many semaphores {self.nsem}"

        def token(o):
            if o.is_dma:
                ch = (o.lane_n - 1) // DMA_CHUNK
                return lane_sems[o.lane][ch], ((o.lane_n - 1) % DMA_CHUNK + 1) * 16, ("L", o.lane, ch)
            ch = (o.count - 1) // SEM_CHUNK
            return eng_sems[o.eng][ch], (o.count - 1) % SEM_CHUNK + 1, ("E", o.eng, ch)

        block = st.enter_context(nc.Block())
        handles = {"pe": "tensor", "act": "scalar", "dve": "vector", "pool": "gpsimd", "sp": "sync"}
        self.nwaits = 0

        def make(e):
            def body(eng):
                waited = {}
                for o in self.ops[e]:
                    need = {}
                    for d in o.deps:
                        sem, val, key = token(d)
                        if waited.get(key, 0) >= val:
                            continue
                        if key not in need or need[key][1] < val:
                            need[key] = (sem, val)
                    for key, (sem, val) in need.items():
                        eng.wait_ge(sem, val)
                        waited[key] = val
                        self.nwaits += 1
                    name, a, k = o.fn
                    ins = getattr(eng, name)(*a, **k)
                    if o.is_dma:
                        sem, val, _ = token(o)
                        ins.then_inc(sem, 16)
                    elif o.flag:
                        sem, val, _ = token(o)
                        ins.then_inc(sem, 1)
                if e == "sp":
                    for o in final_waits:
                        sem, val, key = token(o)
                        eng.wait_ge(sem, val)
            return body

        for e in ENGS:
            getattr(block, handles[e])(make(e))
        st.close()


D = 4096
S = 4096
NTILE = 32
Q0 = 15
NQ = 17
NQ1 = 16
A_IN = 19552
SC_A = 1.0 / math.sqrt(128.0)
SC_B = 1.0 / math.sqrt(64.0)


def build(stop=99, dbg=False):
    nc = bass.Bass("TRN2", target_bir_lowering=False)
    P = Prog(nc)

    def din(name, shape, dt):
        return nc.dram_tensor(name, list(shape), dt, kind="ExternalInput").ap()

    def dscr(name, shape, dt):
        kind = "ExternalOutput" if (dbg and name in dbg) else "Internal"
        return nc.dram_tensor(name, list(shape), dt, kind=kind).ap()

    xbuf = din("xbuf", [NTILE, 128, D], F32)
    pbuf = din("pbuf", [2, NQ, 128, 256], F32)
    a_w_in = din("a_w_in", [D, A_IN], F32)
    a_w_out = din("a_w_out", [D, D], F32)
    w1k = din("w1k", [4096, 256], F32)
    w1v = din("w1v", [4096, 256], F32)
    w2k = din("w2k", [256, 128], F32)
    w2v = din("w2v", [256, 128], F32)
    posTk = din("posTk", [128, 32], F32)
    posTv = din("posTv", [128, 32], F32)
    w_kv = din("w_kv", [D, 1024], F32)
    b_w_in = din("b_w_in", [D, 8192], F32)
    b_w_out = din("b_w_out", [D, D], F32)
    gate_w = din("gate_w", [2, D, D], F32)
    ple_proj = din("ple_proj", [2, 256, D], F32)
    gains = din("gains", [6, D], F32)
    sinks = din("sinks", [64], F32)
    c_ident = din("c_ident", [128, 128], BF16)
    c_mle = din("c_mle", [128, 128], BF16)
    c_mgt = din("c_mgt", [128, 128], BF16)
    c_emat = din("c_emat", [64, 4096], BF16)
    c_m2s = din("c_m2s", [128, 2, 64], BF16)
    c_kval = din("c_kval", [128, NTILE, 8], BF16)
    c_cosA = din("c_cosA", [128, NTILE, 16], F32)
    c_sinA = din("c_sinA", [128, NTILE, 16], F32)
    c_cosB = din("c_cosB", [128, NTILE, 8], F32)
    c_sinB = din("c_sinB", [128, NTILE, 8], F32)
    c_keep = din("c_keep", [128, NQ, 64], F32)
    c_add = din("c_add", [128, NQ, 64], F32)
    c_cmask = din("c_cmask", [128, NQ, 2, 128], BF16)
    out = nc.dram_tensor("out", [NQ1, 128, D], F32, kind="ExternalOutput").ap()

    QTu = dscr("QTu", [NQ, 128, 32, 128], BF16)
    QTr = dscr("QTr", [NQ, 128, 32, 128], BF16)
    KTs = dscr("KTs", [4, 128, S], BF16)
    KTw = dscr("KTw", [4, 128, S], BF16)
    KcT = dscr("KcT", [4, 128, S], BF16)
    VcT = dscr("VcT", [4, 128, S], BF16)
    Vs = dscr("Vs", [NTILE, 128, 4, 129], BF16)
    Vw = dscr("Vw", [NTILE, 128, 4, 129], BF16)
    GT = dscr("GT", [NQ, 128, 96], F32)
    ZS = dscr("ZS", [NQ, 128, 12288], F32)
    MIXT = dscr("MIXT", [NQ, 128, 32, 128], BF16)
    H1 = dscr("H1", [NQ, 128, D], F32)
    H2 = dscr("H2", [NQ, 128, D], F32)
    KT1 = dscr("KT1", [8, 64, NQ * 128], BF16)
    V1 = dscr("V1", [NQ, 128, 8, 65], BF16)
    QT1 = dscr("QT1", [NQ1, 8, 64, 8, 128], BF16)
    Z1 = dscr("Z1", [NQ1, 128, D], F32)
    MIXT1 = dscr("MIXT1", [NQ1, 128, 32, 128], BF16)
    H3 = dscr("H3", [NQ1, 128, D], F32)
    H4 = dscr("H4", [NQ1, 128, D], F32)

    idt = P.sbuf("idt", [128, 128], BF16)
    mle = P.sbuf("mle", [128, 128], BF16)
    mgt = P.sbuf("mgt", [128, 128], BF16)
    kval = P.sbuf("kval", [128, NTILE, 8], BF16)
    cosA = P.sbuf("cosA", [128, NTILE, 16], F32)
    sinA = P.sbuf("sinA", [128, NTILE, 16], F32)
    cosB = P.sbuf("cosB", [128, NTILE, 8], F32)
    sinB = P.sbuf("sinB", [128, NTILE, 8], F32)
    kcmpT = P.sbuf("kcmpT", [128, 4, 256], BF16)
    vcmp = P.sbuf("vcmp", [128, 4, 2, 129], BF16)
    ss = P.sbuf("ss", [128, 8], F32)
    ssq = [P.sbuf(f"ssq{i}", [128, 8], F32) for i in range(2)]
    rstd = [P.sbuf(f"rstd{i}", [128, 8], F32) for i in range(2)]
    rtmp2 = P.sbuf("rtmp2", [128, 16], F32)
    sm = P.sbuf("sm", [128, 64], F32)
    for t_, s_, nm in ((idt, c_ident, "idt"), (mle, c_mle, "mle"), (mgt, c_mgt, "mgt"), (kval, c_kval, "kval"),
                       (cosA, c_cosA, "cosA"), (sinA, c_sinA, "sinA"), (cosB, c_cosB, "cosB"), (sinB, c_sinB, "sinB")):
        P.op("sp", lambda e: e.dma_start(out=t_[:], in_=s_), w=[nm], lane=nm)
    ARENA_BYTES = 176 * 1024
    AR = Arena(P.sbuf("arena", [128, ARENA_BYTES // 2], BF16), ARENA_BYTES)
    ps = [P.psum(f"ps{i}", [128, 512], F32) for i in range(8)]
    psb = [p[:].bitcast(BF16) for p in ps]

    NTB = 5
    dn = {}

    def dense_layout():
        AR.reset()
        dn["xb"] = [AR.alloc([128, D], F32) for _ in range(2)]
        dn["gbc"] = AR.alloc([128, D], F32)
        dn["xs"] = [AR.alloc([128, D], BF16) for _ in range(2)]
        dn["hnT"] = AR.alloc([128, 32, NTB * 128], BF16)
        dn["wb"] = [AR.alloc([128, 8, 512], BF16) for _ in range(5)]
        dn["ub"] = [AR.alloc([128, 512], BF16) for _ in range(2)]
        dn["rb"] = [AR.alloc([128, 512], BF16) for _ in range(2)]
        dn["rtmp"] = [AR.alloc([128, 128], F32) for _ in range(4)]
        dn["tst"] = [AR.alloc([128, 8, 128], BF16) for _ in range(2)]
        dn["vst"] = [AR.alloc([128, 4, 129], BF16) for _ in range(2)]
        dn["v1st"] = [AR.alloc([128, 8, 65], BF16) for _ in range(2)]
        dn["fst"] = [AR.alloc([128, 512], F32) for _ in range(2)]
        dn["pT"] = AR.alloc([128, 2, NTB * 128], BF16)
        dn["pws"] = [AR.alloc([128, 2, 512], BF16) for _ in range(2)]
        dn["pxb"] = AR.alloc([128, 256], F32)
        dn["pxh"] = AR.alloc([128, 256], BF16)
        dn["rsb"] = [AR.alloc([128, 512], F32) for _ in range(2)]

    def load_gain(gi):
        P.op("sp", lambda e: e.dma_start(out=dn["gbc"], in_=gains[gi].partition_broadcast(128)), w=["gbc"], lane="gbc")

    def norm_stage(src, deps, par, i):
        s = P.nxt("xb", 2)
        xb_, xs, gbc = dn["xb"][s], dn["xs"][s], dn["gbc"]
        P.op("sp", lambda e: e.dma_start(out=xb_, in_=src), r=list(deps), w=[f"xb{s}"], lane=f"xb{s}")
        P.op("act", lambda e: e.activation(out=xs, in_=xb_, func=AF.Square, accum_out=ssq[par][:, i:i + 1]), r=[f"xb{s}"], w=[f"xs{s}", ("ssq", par, i)])
        return s

    def norm_stage2(s, i):
        xb_, xs, gbc = dn["xb"][s], dn["xs"][s], dn["gbc"]
        P.op("dve", lambda e: e.tensor_tensor(out=xs, in0=xb_, in1=gbc, op=ALU.mult), r=[f"xb{s}", "gbc"], w=[f"xs{s}"])
        transpose_rows(xs, f"xs{s}", i)

    def rstd_block(par, ntl):
        P.op("dve", lambda e: e.tensor_scalar(out=rtmp2[:, 0:ntl], in0=ssq[par][:, 0:ntl], scalar1=1.0 / D, scalar2=1e-6, op0=ALU.mult, op1=ALU.add),
             r=[("ssq", par, i) for i in range(ntl)], w=["rtmp2"])
        P.op("act", lambda e: e.activation(out=rtmp2[:, 8:8 + ntl], in_=rtmp2[:, 0:ntl], func=AF.Sqrt), r=["rtmp2"], w=["rtmp2b"])
        P.op("dve", lambda e: e.reciprocal(out=rstd[par][:, 0:ntl], in_=rtmp2[:, 8:8 + ntl]), r=["rtmp2b"], w=[("rstd", par)])

    cur = {"par": 0, "scale": False}

    def rs(i):
        return rstd[cur["par"]][:, i:i + 1] if cur["scale"] else 1.0

    def rsdep():
        return [("rstd", cur["par"])] if cur["scale"] else []

    class NormPipe:
        def __init__(self, blocks, src_of):
            self.blocks = blocks
            self.src_of = src_of
            self.staged = {}
            self.par0 = P.nxt("normpar", 2)
            self.A(0, 0)
            self.A(0, 1)

        def par(self, bi):
            return (self.par0 + bi) % 2

        def A(self, bi, i):
            if bi < len(self.blocks) and i < len(self.blocks[bi]) and (bi, i) not in self.staged:
                src, deps = self.src_of(self.blocks[bi][i])
                self.staged[(bi, i)] = norm_stage(src, deps, self.par(bi), i)

        def block(self, bi, per_tile=None):
            blk = self.blocks[bi]
            self.A(bi, 0)
            self.A(bi, 1)
            for i in range(len(blk)):
                norm_stage2(self.staged[(bi, i)], i)
                if per_tile is not None:
                    per_tile(i, blk[i])
                self.A(bi, i + 2)
            rstd_block(self.par(bi), len(blk))
            cur["par"] = self.par(bi)
            cur["scale"] = True
            self.A(bi + 1, 0)
            self.A(bi + 1, 1)

    def transpose_rows(srcbf, sname, i):
        hnT = dn["hnT"]
        for q in range(4):
            bank = (0, 6)[P.nxt("tb", 2)]
            for j in range(8):
                kc = q * 8 + j
                P.op("pe", lambda e: e.transpose(out=psb[bank][:, j * 128:(j + 1) * 128], in_=srcbf[:, kc * 128:(kc + 1) * 128], identity=idt[:]),
                     r=[sname, "idt"], w=[f"ps{bank}"])
            src = psb[bank].rearrange("p (j c) -> p j c", j=8)
            dst = hnT[:, q * 8:(q + 1) * 8, i * 128:(i + 1) * 128]
            if q % 2 == 0:
                P.op("act", lambda e: e.copy(out=dst, in_=src), r=[f"ps{bank}"], w=[f"hnT{i}"])
            else:
                P.op("dve", lambda e: e.tensor_copy(out=dst, in_=src), r=[f"ps{bank}"], w=[f"hnT{i}"])

    def dense(ntl, W, col_blocks, epilogue):
        hnT, wb = dn["hnT"], dn["wb"]
        wv = W.rearrange("(kc p) c -> p kc c", p=128)
        for cbi, (c0, n) in enumerate(col_blocks):
            slots = []
            for part in range(4):
                s = P.nxt("wb", 5)
                slots.append(s)
                P.op("pool", lambda e: e.dma_start(out=wb[s][:, :, 0:n], in_=wv[:, part * 8:(part + 1) * 8, c0:c0 + n]), w=[f"wb{s}"], lane=f"wb{s}")
            ga = 2 if ntl >= 4 else (1 if ntl >= 2 else ntl)
            for grp, banks in ((list(range(0, ga)), (1, 2)), (list(range(ga, ntl)), (3, 4, 5))):
                if not grp:
                    continue
                for part in range(4):
                    s = slots[part]
                    for gi, i in enumerate(grp):
                        bank = banks[gi]
                        for k8 in range(8):
                            kc = part * 8 + k8
                            P.op("pe", lambda e: e.matmul(ps[bank][:, 0:n], lhsT=hnT[:, kc, i * 128:(i + 1) * 128], rhs=wb[s][:, k8, 0:n], start=(kc == 0), stop=(kc == 31)),
                                 r=[f"hnT{i}", f"wb{s}"], w=[f"ps{bank}"])
                for gi, i in enumerate(grp):
                    epilogue(cbi, i, banks[gi])

    def epi_T(bank, tile, nh, hd, rope, want_u, want_r, store, i):
        half = hd // 8
        ub, rb, rtmp, tst = dn["ub"], dn["rb"], dn["rtmp"], dn["tst"]
        psv = ps[bank][:].rearrange("p (h d) -> p h d", h=nh)
        srcs = []
        if want_u:
            u = P.nxt("ub", 2)
            P.op("act", lambda e: e.activation(out=ub[u], in_=ps[bank][:], func=AF.Copy, scale=rs(i)), r=[f"ps{bank}"] + rsdep(), w=[f"ub{u}"])
            srcs.append((ub[u], f"ub{u}"))
        if want_r:
            r_ = P.nxt("rb", 2)
            P.op("act", lambda e: e.activation(out=rb[r_], in_=ps[bank][:], func=AF.Copy, scale=rs(i)), r=[f"ps{bank}"] + rsdep(), w=[f"rb{r_}"])
            cs, sn = (cosA, sinA) if rope == "A" else (cosB, sinB)
            cname, sname = ("cosA", "sinA") if rope == "A" else ("cosB", "sinB")
            cb_ = cs[:, tile:tile + 1, :].broadcast_to([128, nh, half])
            sb_ = sn[:, tile:tile + 1, :].broadcast_to([128, nh, half])
            t1 = psv[:, :, 0:half]
            t2 = psv[:, :, half:2 * half]
            tm = [rtmp[k][:, 0:nh * half].rearrange("p (h d) -> p h d", h=nh) for k in range(4)]
            P.op("dve", lambda e: e.scalar_tensor_tensor(out=tm[0], in0=t1, scalar=rs(i), in1=cb_, op0=ALU.mult, op1=ALU.mult), r=[f"ps{bank}", cname] + rsdep(), w=["rt0"])
            P.op("dve", lambda e: e.scalar_tensor_tensor(out=tm[1], in0=t2, scalar=rs(i), in1=sb_, op0=ALU.mult, op1=ALU.mult), r=[f"ps{bank}", sname] + rsdep(), w=["rt1"])
            P.op("dve", lambda e: e.scalar_tensor_tensor(out=tm[2], in0=t1, scalar=rs(i), in1=sb_, op0=ALU.mult, op1=ALU.mult), r=[f"ps{bank}", sname] + rsdep(), w=["rt2"])
            P.op("dve", lambda e: e.scalar_tensor_tensor(out=tm[3], in0=t2, scalar=rs(i), in1=cb_, op0=ALU.mult, op1=ALU.mult), r=[f"ps{bank}", cname] + rsdep(), w=["rt3"])
            rbv = rb[r_].rearrange("p (h d) -> p h d", h=nh)
            P.op("dve", lambda e: e.tensor_tensor(out=rbv[:, :, 0:half], in0=tm[0], in1=tm[1], op=ALU.subtract), r=["rt0", "rt1"], w=[f"rb{r_}"])
            P.op("dve", lambda e: e.tensor_tensor(out=rbv[:, :, half:2 * half], in0=tm[2], in1=tm[3], op=ALU.add), r=["rt2", "rt3"], w=[f"rb{r_}"])
            srcs.append((rb[r_], f"rb{r_}"))
        tb = (7, 0, 6)[P.nxt("eb", 3)]
        slot = 0
        for (sb, sname_) in srcs:
            for h in range(nh):
                P.op("pe", lambda e: e.transpose(out=psb[tb][0:hd, slot * 128:(slot + 1) * 128], in_=sb[:, h * hd:(h + 1) * hd], identity=idt[:]),
                     r=[sname_, "idt"], w=[f"ps{tb}"])
                slot += 1
        st_ = P.nxt("tst", 2)
        nsl = slot
        P.op("dve", lambda e: e.tensor_copy(out=tst[st_][0:hd, 0:nsl, :], in_=psb[tb][0:hd, 0:nsl * 128].rearrange("p (j c) -> p j c", j=nsl)),
             r=[f"ps{tb}"], w=[f"tst{st_}"])
        store(tst[st_], f"tst{st_}")

    def epi_V(bank, tile, dst, key, i):
        vst = dn["vst"]
        v = P.nxt("vst", 2)
        P.op("act", lambda e: e.activation(out=vst[v][:, :, 0:128], in_=ps[bank][:].rearrange("p (h d) -> p h d", h=4), func=AF.Copy, scale=rs(i)), r=[f"ps{bank}"] + rsdep(), w=[f"vst{v}"])
        P.op("dve", lambda e: e.tensor_copy(out=vst[v][:, :, 128:129], in_=kval[:, tile, 0:4].unsqueeze(2)), r=["kval", f"vst{v}"], w=[f"vst{v}"])
        P.op("sp", lambda e: e.dma_start(out=dst, in_=vst[v]), r=[f"vst{v}"], w=[key], lane=f"vst{v}")

    def epi_F(bank, func, n, dst, key, i):
        fst = dn["fst"]
        f = P.nxt("fst", 2)
        P.op("act", lambda e: e.activation(out=fst[f][:, 0:n], in_=ps[bank][:, 0:n], func=func, scale=rs(i)), r=[f"ps{bank}"] + rsdep(), w=[f"fst{f}"])
        P.op("sp", lambda e: e.dma_start(out=dst, in_=fst[f][:, 0:n]), r=[f"fst{f}"], w=[key], lane=f"fst{f}")

    def finish_early():
        fw = [P.last_w[k] for k in list(P.last_w) if isinstance(k, tuple)]
        for o in fw:
            o.flag = True
        P.emit(final_waits=fw)
        return nc, P

    dense_layout()
    load_gain(0)
    KV_BLOCKS = [(4096 + 512 * k, 512) for k in range(6)]
    Q_BLOCKS = [(512 * k, 512) for k in range(8)]
    G_BLOCK = [(7168, 96)]
    Z_BLOCKS = [(7264 + 512 * k, 512) for k in range(24)]

    def p1_block(t0, ntl, own):
        p1_pipe.block(p1_index[t0])
        blocks = (Q_BLOCKS if own else []) + KV_BLOCKS + ((G_BLOCK + Z_BLOCKS) if own else [])
        nq = 8 if own else 0

        def epi(cbi, i, bank):
            tile = t0 + i
            lt = tile - Q0
            if cbi < nq:
                def store(stg, key):
                    P.op("sp", lambda e: e.dma_start(out=QTu[lt][:, 4 * cbi:4 * cbi + 4, :], in_=stg[:, 0:4, :]), r=[key], w=[("QTu", lt, cbi)], lane=key + "a")
                    P.op("sp", lambda e: e.dma_start(out=QTr[lt][:, 4 * cbi:4 * cbi + 4, :], in_=stg[:, 4:8, :]), r=[key], w=[("QTr", lt, cbi)], lane=key + "b")
                epi_T(bank, tile, 4, 128, "A", True, True, store, i)
            elif cbi < nq + 6:
                k = cbi - nq
                if k in (0, 1, 2, 4):
                    dstT, nm = {0: (KcT, "KcT"), 1: (VcT, "VcT"), 2: (KTs, "KTs"), 4: (KTw, "KTw")}[k]

                    def store(stg, key):
                        P.op("sp", lambda e: e.dma_start(out=dstT[:, :, tile * 128:(tile + 1) * 128].rearrange("g p t -> p g t"), in_=stg[:, 0:4, :]),
                             r=[key], w=[(nm, tile)], lane=key + "a")
                    if k < 2:
                        epi_T(bank, tile, 4, 128, None, True, False, store, i)
                    else:
                        epi_T(bank, tile, 4, 128, "A", False, True, store, i)
                else:
                    dstV = Vs if k == 3 else Vw
                    epi_V(bank, tile, dstV[tile], ("Vs" if k == 3 else "Vw", tile), i)
            elif cbi == nq + 6:
                epi_F(bank, AF.Sigmoid, 96, GT[lt], ("GT", lt), i)
            else:
                zc = cbi - nq - 7
                epi_F(bank, AF.Silu, 512, ZS[lt][:, zc * 512:(zc + 1) * 512], ("ZS", lt, zc), i)

        dense(ntl, a_w_in, blocks, epi)

    P1_BLOCKS = [(0, 5, False), (5, 5, False), (10, 5, False), (15, 5, True), (20, 4, True), (24, 4, True), (28, 4, True)]
    p1_index = {t0: bi for bi, (t0, _, _) in enumerate(P1_BLOCKS)}
    p1_pipe = NormPipe([list(range(t0, t0 + n_)) for (t0, n_, _) in P1_BLOCKS], lambda tile: (xbuf[tile], ()))
    for (t0, ntl, own) in P1_BLOCKS:
        p1_block(t0, ntl, own)
    if stop < 2:
        return finish_early()

    P.barrier()
    AR.reset()
    w1b = AR.alloc([128, 32, 256], BF16)
    w2b = AR.alloc([128, 2, 128], BF16)
    posb = AR.alloc([128, 32], BF16)
    hb = AR.alloc([128, 2], F32)
    hT = AR.alloc([128, 2, 256], BF16)
    kct = AR.alloc([128, 256, 16], BF16)
    kctd = AR.alloc([128, 16, 256], BF16)
    P.op("dve", lambda e: e.memset(hT, 0.0), w=["hT"])
    P.op("dve", lambda e: e.memset(vcmp[:], 1.0), w=["vcmp"])
    for kv in range(2):
        w1, w2, posT, srcT, nm = (w1k, w2k, posTk, KcT, "KcT") if kv == 0 else (w1v, w2v, posTv, VcT, "VcT")
        P.op("pool", lambda e: e.dma_start(out=w1b, in_=w1.rearrange("(l p) c -> p l c", p=128)), w=["w1b"], lane="w1b")
        P.op("pool", lambda e: e.dma_start(out=w2b, in_=w2.rearrange("(c p) d -> p c d", p=128)), w=["w2b"], lane="w2b")
        P.op("pool", lambda e: e.dma_start(out=posb, in_=posT), w=["posb"], lane="posb")
        for hc in range(2):
            for l in range(32):
                P.op("pe", lambda e: e.matmul(ps[0][:, hc:hc + 1], lhsT=w1b[:, l, hc * 128:(hc + 1) * 128], rhs=posb[:, l:l + 1], start=(l == 0), stop=(l == 31)),
                     r=["w1b", "posb"], w=["ps0"])
            P.op("dve", lambda e: e.tensor_copy(out=hb[:, hc:hc + 1], in_=ps[0][:, hc:hc + 1]), r=["ps0"], w=["hb"])
        for g in range(4):
            P.op("sp", lambda e: e.dma_start(out=kct.rearrange("p n s -> p (n s)"), in_=srcT[g]), r=[(nm, t) for t in range(NTILE)], w=["kct"], lane="kct")
            P.op("dve", lambda e: e.tensor_copy(out=kctd, in_=kct.rearrange("p n s -> p s n")), r=["kct"], w=["kctd"])
            for hc in range(2):
                bank = 2 + hc
                for l in range(32):
                    P.op("pe", lambda e: e.matmul(ps[bank][:, 0:255], lhsT=w1b[:, l, hc * 128:(hc + 1) * 128], rhs=kctd[:, l % 16, l // 16:l // 16 + 255], start=(l == 0), stop=(l == 31)),
                         r=["w1b", "kctd"], w=[f"ps{bank}"])
                P.op("act", lambda e: e.activation(out=hT[:, hc, 0:255], in_=ps[bank][:, 0:255], func=AF.Silu, bias=hb[:, hc:hc + 1]),
                     r=[f"ps{bank}", "hb"], w=["hT"])
            if kv == 0:
                for hc in range(2):
                    P.op("pe", lambda e: e.matmul(ps[4][:, 0:256], lhsT=w2b[:, hc, :], rhs=hT[:, hc, :], start=(hc == 0), stop=(hc == 1)), r=["w2b", "hT"], w=["ps4"])
                P.op("dve", lambda e: e.tensor_copy(out=kcmpT[:, g, :], in_=ps[4][:, 0:256]), r=["ps4"], w=["kcmpT"])
            else:
                for nt_ in range(2):
                    for hc in range(2):
                        P.op("pe", lambda e: e.matmul(ps[4 + nt_][:, 0:128], lhsT=hT[:, hc, nt_ * 128:(nt_ + 1) * 128], rhs=w2b[:, hc, :], start=(hc == 0), stop=(hc == 1)),
                             r=["w2b", "hT"], w=[f"ps{4 + nt_}"])
                    P.op("dve", lambda e: e.tensor_copy(out=vcmp[:, g, nt_, 0:128], in_=ps[4 + nt_][:, 0:128]), r=[f"ps{4 + nt_}"], w=["vcmp"])
    if stop < 3:
        return finish_early()

    P.barrier()
    AR.reset()
    emat = AR.alloc([64, 4096], BF16)
    m2s = AR.alloc([128, 2, 64], BF16)
    keep = AR.alloc([128, NQ, 64], F32)
    addm = AR.alloc([128, NQ, 64], F32)
    cmask = AR.alloc([128, NQ, 2, 128], BF16)
    for t_, s_, nm in ((emat, c_emat, "emat"), (m2s, c_m2s, "m2s"), (keep, c_keep, "keep"), (addm, c_add, "addm"), (cmask, c_cmask, "cmask")):
        P.op("sp", lambda e: e.dma_start(out=t_, in_=s_), w=[nm], lane=nm)
    kts = AR.alloc([128, S], BF16)
    ktw = AR.alloc([128, S], BF16)
    vsb = AR.alloc([128, NTILE, 129], BF16)
    vwb = AR.alloc([128, NTILE, 129], BF16)
    qu = [AR.alloc([128, 8, 128], BF16) for _ in range(2)]
    qr = [AR.alloc([128, 8, 128], BF16) for _ in range(2)]
    gt = [AR.alloc([128, 96], F32) for _ in range(2)]
    zsb = [AR.alloc([128, 3, 1024], F32) for _ in range(2)]
    pbh = [AR.alloc([128, 4, 128], BF16) for _ in range(6)]
    smk = [AR.alloc([128, 128], BF16) for _ in range(3)]
    selT = AR.alloc([64, 128], BF16)
    selm = AR.alloc([128, 64], BF16)
    impt = AR.alloc([128, 8, 64], F32)
    imp = AR.alloc([128, 4, 64], F32)
    m8 = AR.alloc([128, 16], F32)
    mixa2 = [AR.alloc([128, 8, 128], F32) for _ in range(2)]
    mixb = AR.alloc([128, 8, 128], F32)
    osb2 = [AR.alloc([128, 3, 387], F32) for _ in range(2)]
    mixh = AR.alloc([128, 1024], BF16)
    tst3 = [AR.alloc([128, 8, 128], BF16) for _ in range(2)]
    OB = (4, 5, 6)

    def ov(h):
        return ps[OB[h // 3]][:, (h % 3) * 129:(h % 3) * 129 + 129]

    def att_s1(kT, kname, qsrc, qname, hf, mask, scale):
        sb = P.nxt("sbk", 4)
        P.op("pe", lambda e: e.matmul(ps[sb][:], lhsT=kT, rhs=qsrc[:, 4 * hf:4 * hf + 4, :], start=True, stop=True),
             r=[kname, qname], w=[f"ps{sb}"])
        p_ = P.nxt("pbh", 6)
        P.op("act", lambda e: e.activation(out=pbh[p_], in_=ps[sb][:].rearrange("p (h t) -> p h t", h=4), func=AF.Exp, scale=scale),
             r=[f"ps{sb}"], w=[f"pbh{p_}"])
        if mask is not None:
            mk, mname = mask
            P.op("dve", lambda e: e.tensor_tensor(out=pbh[p_], in0=pbh[p_], in1=mk.unsqueeze(1).broadcast_to([128, 4, 128]), op=ALU.mult),
                 r=[f"pbh{p_}", mname], w=[f"pbh{p_}"])
        return p_

    def att_s2(p_, hf, vtile, vname, first, extra=None):
        for hh in range(4):
            h = 4 * hf + hh
            st_flag = first and (h % 3 == 0)
            P.op("pe", lambda e: e.matmul(ov(h), lhsT=pbh[p_][:, hh, :], rhs=vtile, start=st_flag, stop=False, skip_group_check=True),
                 r=[f"pbh{p_}", vname], w=[f"ps{OB[h // 3]}"])
        if extra is not None:
            extra(p_, hf)

    def run_branch(items, look=3, hooks=None):
        pend = []
        for n_, it in enumerate(items):
            mask = it["mask"]() if callable(it["mask"]) else it["mask"]
            for hf in range(2):
                p_ = att_s1(it["kT"], it["kname"], it["q"], it["qname"], hf, mask, SC_A)
                pend.append((p_, hf, it["v"], it["vname"], n_ == 0, it.get("extra")))
                if len(pend) > look:
                    att_s2(*pend.pop(0))
            if hooks and n_ in hooks:
                hooks[n_]()
        while pend:
            att_s2(*pend.pop(0))

    def fin1():
        ob = P.nxt("osb", 2)
        osb = osb2[ob]
        for bi in range(3):
            nh_ = 3 if bi < 2 else 2
            P.op("act", lambda e: e.copy(out=osb[:, bi, 0:nh_ * 129], in_=ps[OB[bi]][:, 0:nh_ * 129]), r=[f"ps{OB[bi]}"], w=[("osb", ob, bi)])
        for bi in range(3):
            nh_ = 3 if bi < 2 else 2
            P.op("dve", lambda e: e.tensor_scalar(out=sm[:, 32 * ob + 3 * bi:32 * ob + 3 * bi + nh_], in0=osb[:, bi, 0:nh_ * 129].rearrange("p (h d) -> p h d", d=129)[:, :, 128],
                                                   scalar1=1e-30, scalar2=None, op0=ALU.max),
                 r=[("osb", ob, bi)], w=[("smrs", ob, bi)])
        P.op("dve", lambda e: e.reciprocal(out=sm[:, 32 * ob + 8:32 * ob + 16], in_=sm[:, 32 * ob:32 * ob + 8]), r=[("smrs", ob, 0), ("smrs", ob, 1), ("smrs", ob, 2)], w=[("sm_ri", ob)])
        return ob

    def fin2(ob, br, g, gtt, gname, zt, zname, first_branch, mi_):
        osb = osb2[ob]
        mixa = mixa2[mi_]
        man = f"mixa{mi_}"
        P.op("dve", lambda e: e.tensor_tensor(out=sm[:, 32 * ob + 16:32 * ob + 24], in0=sm[:, 32 * ob + 8:32 * ob + 16], in1=gtt[:, br * 32 + g * 8:br * 32 + g * 8 + 8], op=ALU.mult),
             r=[("sm_ri", ob), gname], w=[("sm_f", ob)])
        dst = mixa if first_branch else mixb
        dname = man if first_branch else "mixb"
        for bi in range(3):
            nh_ = 3 if bi < 2 else 2
            P.op("dve", lambda e: e.tensor_tensor(out=dst[:, 3 * bi:3 * bi + nh_, :], in0=osb[:, bi, 0:nh_ * 129].rearrange("p (h d) -> p h d", d=129)[:, :, 0:128],
                                                   in1=sm[:, 32 * ob + 16 + 3 * bi:32 * ob + 16 + 3 * bi + nh_].unsqueeze(2).broadcast_to([128, nh_, 128]), op=ALU.mult),
                 r=[("osb", ob, bi), ("sm_f", ob)], w=[(dname, bi)])
        zv = zt[:, br, :].rearrange("p (h d) -> p h d", h=8)
        allb = [(dname, 0), (dname, 1), (dname, 2)]
        P.op("dve", lambda e: e.tensor_tensor(out=dst, in0=dst, in1=zv, op=ALU.mult), r=allb + [zname], w=allb)
        if not first_branch:
            alla = [(man, 0), (man, 1), (man, 2)]
            P.op("dve", lambda e: e.tensor_tensor(out=mixa, in0=mixa, in1=mixb, op=ALU.add), r=alla + allb, w=alla)

    def mix_tail(n):
        g, qt = iters[n]
        mi_ = n % 2
        man = f"mixa{mi_}"
        P.op("act", lambda e: e.copy(out=mixh, in_=mixa2[mi_].rearrange("p h d -> p (h d)")), r=[(man, 0), (man, 1), (man, 2)], w=["mixh"])
        for h in range(8):
            P.op("pe", lambda e: e.transpose(out=psb[7][:, h * 128:(h + 1) * 128], in_=mixh[:, h * 128:(h + 1) * 128], identity=idt[:]), r=["mixh", "idt"], w=["ps7"])
        st_ = P.nxt("tst3", 2)
        P.op("dve", lambda e: e.tensor_copy(out=tst3[st_], in_=psb[7][:].rearrange("p (j c) -> p j c", j=8)), r=["ps7"], w=[f"tst3{st_}"])
        P.op("sp", lambda e: e.dma_start(out=MIXT[qt][:, g * 8:(g + 1) * 8, :], in_=tst3[st_]), r=[f"tst3{st_}"], w=[("MIXT", qt, g)], lane=f"tst3{st_}")

    iters = [(g, qt) for g in range(4) for qt in range(NQ)]
    slot_of = {}

    def att_loads(n):
        g, qt = iters[n]
        qs = P.nxt("qs", 2)
        slot_of[n] = qs
        P.op("sp", lambda e: e.dma_start(out=qu[qs], in_=QTu[qt][:, g * 8:(g + 1) * 8, :]), r=[("QTu", qt, 2 * g), ("QTu", qt, 2 * g + 1)], w=[f"qu{qs}"], lane=f"qu{qs}")
        P.op("sp", lambda e: e.dma_start(out=qr[qs], in_=QTr[qt][:, g * 8:(g + 1) * 8, :]), r=[("QTr", qt, 2 * g), ("QTr", qt, 2 * g + 1)], w=[f"qr{qs}"], lane=f"qr{qs}")
        P.op("sp", lambda e: e.dma_start(out=gt[qs], in_=GT[qt]), r=[("GT", qt)], w=[f"gt{qs}"], lane=f"gt{qs}")
        P.op("sp", lambda e: e.dma_start(out=zsb[qs], in_=ZS[qt].rearrange("p (b c) -> p b c", b=3)[:, :, g * 1024:(g + 1) * 1024]),
             r=[("ZS", qt, zc) for zc in range(24)], w=[f"zsb{qs}"], lane=f"zsb{qs}")

    att_loads(0)
    for n, (g, qt) in enumerate(iters):
        if qt == 0:
            P.op("sp", lambda e: e.dma_start(out=kts, in_=KTs[g]), r=[("KTs", t) for t in range(NTILE)], w=["kts"], lane="kts")
            P.op("sp", lambda e: e.dma_start(out=ktw, in_=KTw[g]), r=[("KTw", t) for t in range(NTILE)], w=["ktw"], lane="ktw")
            P.op("sp", lambda e: e.dma_start(out=vsb, in_=Vs[:, :, g, :].rearrange("n p d -> p n d")), r=[("Vs", t) for t in range(NTILE)], w=["vsb"], lane="vsb")
            P.op("sp", lambda e: e.dma_start(out=vwb, in_=Vw[:, :, g, :].rearrange("n p d -> p n d")), r=[("Vw", t) for t in range(NTILE)], w=["vwb"], lane="vwb")
        if n + 1 < len(iters):
            att_loads(n + 1)
        if True:
            j = Q0 + qt
            qs = slot_of[n]
            fargs = (g, gt[qs], f"gt{qs}", zsb[qs], f"zsb{qs}")

            def imp_mm(p_, hf, nt_):
                for hh in range(4):
                    h = 4 * hf + hh
                    P.op("pe", lambda e: e.matmul(ps[7][:, h * 64:(h + 1) * 64], lhsT=pbh[p_][:, hh, :], rhs=m2s[:, nt_, :], start=(nt_ == 0 and h == 0), stop=False, skip_group_check=True),
                         r=[f"pbh{p_}", "m2s"], w=["ps7"])
            items = []
            for nt_ in range(2):
                items.append(dict(kT=kcmpT[:, g, nt_ * 128:(nt_ + 1) * 128], kname="kcmpT", q=qu[qs], qname=f"qu{qs}", mask=(cmask[:, qt, nt_, :], "cmask"),
                                  v=vcmp[:, g, nt_, :], vname="vcmp", extra=(lambda p_, hf, nt_=nt_: imp_mm(p_, hf, nt_))))
            run_branch(items)

            def selection_dve(ob):
                P.op("dve", lambda e: e.tensor_tensor(out=impt, in0=ps[7][:].rearrange("p (h j) -> p h j", h=8), in1=sm[:, 32 * ob + 8:32 * ob + 16].unsqueeze(2).broadcast_to([128, 8, 64]), op=ALU.mult),
                     r=["ps7", ("sm_ri", ob)], w=["impt"])
                P.op("dve", lambda e: e.tensor_reduce(out=imp[:, 0, :], in_=impt.rearrange("p h j -> p j h"), axis=AX.X, op=ALU.add), r=["impt"], w=["imp0"])
                P.op("dve", lambda e: e.tensor_tensor(out=imp[:, 0, :], in0=imp[:, 0, :], in1=keep[:, qt, :], op=ALU.mult), r=["imp0", "keep"], w=["imp0"])
                P.op("dve", lambda e: e.tensor_tensor(out=imp[:, 0, :], in0=imp[:, 0, :], in1=addm[:, qt, :], op=ALU.add), r=["imp0", "addm"], w=["imp0"])
                P.op("dve", lambda e: e.max(out=m8[:, 0:8], in_=imp[:, 0, :]), r=["imp0"], w=["m8a"])
                P.op("dve", lambda e: e.match_replace(out=imp[:, 1, :], in_to_replace=m8[:, 0:8], in_values=imp[:, 0, :], imm_value=-2.0), r=["imp0", "m8a"], w=["imp1"])
                P.op("dve", lambda e: e.max(out=m8[:, 8:16], in_=imp[:, 1, :]), r=["imp1"], w=["m8b"])
                P.op("dve", lambda e: e.match_replace(out=imp[:, 2, :], in_to_replace=m8[:, 8:16], in_values=imp[:, 1, :], imm_value=-2.0), r=["imp1", "m8b"], w=["imp2"])
                P.op("dve", lambda e: e.tensor_tensor(out=selm, in0=imp[:, 0, :], in1=imp[:, 2, :], op=ALU.not_equal), r=["imp0", "imp2"], w=["selm"])

            ob_c = fin1()
            items = []
            kts_w = [kt for kt in range(j - 4, j + 1) if kt >= 0]
            for n_, kt in enumerate(kts_w):
                mask = (mle[:], "mle") if kt == j else ((mgt[:], "mgt") if kt == j - 4 else None)
                items.append(dict(kT=ktw[:, kt * 128:(kt + 1) * 128], kname="ktw", q=qr[qs], qname=f"qr{qs}", mask=mask, v=vwb[:, kt, :], vname="vwb"))
            run_branch(items, hooks={0: (lambda ob_c=ob_c: selection_dve(ob_c))})
            fin2(ob_c, 0, *fargs, True, n % 2)
            P.op("pe", lambda e: e.transpose(out=psb[7][0:64, 0:128], in_=selm, identity=idt[:]), r=["selm", "idt"], w=["ps7"])
            P.op("dve", lambda e: e.tensor_copy(out=selT, in_=psb[7][0:64, 0:128]), r=["ps7"], w=["selT"])
            ob_w = fin1()

            def mk_pre(kt):
                def pre():
                    P.op("pe", lambda e: e.matmul(ps[7][:, 0:128], lhsT=emat[:, kt * 128:(kt + 1) * 128], rhs=selT, start=True, stop=True), r=["emat", "selT"], w=["ps7"])
                    mi = P.nxt("smk", 3)
                    if kt == j:
                        P.op("dve", lambda e: e.tensor_tensor(out=smk[mi], in0=ps[7][:, 0:128], in1=mle[:], op=ALU.mult), r=["ps7", "mle"], w=[f"smk{mi}"])
                    else:
                        P.op("dve", lambda e: e.tensor_copy(out=smk[mi], in_=ps[7][:, 0:128]), r=["ps7"], w=[f"smk{mi}"])
                    return (smk[mi], f"smk{mi}")
                return pre
            items = []
            for kt in range(j + 1):
                items.append(dict(kT=kts[:, kt * 128:(kt + 1) * 128], kname="kts", q=qr[qs], qname=f"qr{qs}", mask=mk_pre(kt), v=vsb[:, kt, :], vname="vsb"))

            def hook_sel(n=n, ob_w=ob_w, fargs=fargs):
                fin2(ob_w, 2, *fargs, False, n % 2)
                if n > 0:
                    mix_tail(n - 1)
            run_branch(items, hooks={1: hook_sel})
            ob_s = fin1()
            fin2(ob_s, 1, *fargs, False, n % 2)
    mix_tail(len(iters) - 1)
    if stop < 4:
        return finish_early()

    CB8 = [(512 * k, 512) for k in range(8)]

    def outproj_pass(tiles, MT, mtname, nmt, W, Hsrc_fn, Hdst, hname):
        hnT, rsb = dn["hnT"], dn["rsb"]
        cur["scale"] = False
        for i, (lt, _) in enumerate(tiles):
            P.op("sp", lambda e: e.dma_start(out=hnT[:, :, i * 128:(i + 1) * 128], in_=MT[lt]), r=[(mtname, lt, g) for g in range(nmt)], w=[f"hnT{i}"], lane=f"hnT{i}")

        def epi(cbi, i, bank):
            lt, src = tiles[i]
            fst = dn["fst"]
            f = P.nxt("fst", 2)
            P.op("act", lambda e: e.copy(out=fst[f], in_=ps[bank][:]), r=[f"ps{bank}"], w=[f"fst{f}"])
            r_ = P.nxt("rsb", 2)
            P.op("sp", lambda e: e.dma_start(out=rsb[r_], in_=src[:, cbi * 512:(cbi + 1) * 512]), r=Hsrc_fn(lt, cbi), w=[f"rsb{r_}"], lane=f"rsb{r_}")
            P.op("dve", lambda e: e.tensor_tensor(out=rsb[r_], in0=fst[f], in1=rsb[r_], op=ALU.add), r=[f"fst{f}", f"rsb{r_}"], w=[f"rsb{r_}"])
            P.op("sp", lambda e: e.dma_start(out=Hdst[lt][:, cbi * 512:(cbi + 1) * 512], in_=rsb[r_]), r=[f"rsb{r_}"], w=[(hname, lt, cbi)], lane=f"rsb{r_}s")
        dense(len(tiles), W, CB8, epi)

    def ple_pass(layer, pipe, bi, tiles, Hsrc, hsname, Hdst, hdname, pidx_fn):
        pT, pws, pxb, pxh, rsb, fst = dn["pT"], dn["pws"], dn["pxb"], dn["pxh"], dn["rsb"], dn["fst"]
        def per_tile(i, lt):
            P.op("sp", lambda e: e.dma_start(out=pxb, in_=pbuf[layer, pidx_fn(lt)]), w=["pxb"], lane="pxb")
            P.op("act", lambda e: e.copy(out=pxh, in_=pxb), r=["pxb"], w=["pxh"])
            for c in range(2):
                P.op("pe", lambda e: e.transpose(out=psb[7][:, c * 128:(c + 1) * 128], in_=pxh[:, c * 128:(c + 1) * 128], identity=idt[:]), r=["pxh", "idt"], w=["ps7"])
            P.op("dve", lambda e: e.tensor_copy(out=pT[:, :, i * 128:(i + 1) * 128], in_=psb[7][:, 0:256].rearrange("p (j c) -> p j c", j=2)), r=["ps7"], w=[f"pT{i}"])
        pipe.block(bi, per_tile)
        pwv = ple_proj[layer].rearrange("(c p) d -> p c d", p=128)
        cur = {}

        def epi(cbi, i, bank):
            lt = tiles[i]
            if i == 0:
                w_ = P.nxt("pws", 2)
                cur["w"] = w_
                P.op("pool", lambda e: e.dma_start(out=pws[w_], in_=pwv[:, :, cbi * 512:(cbi + 1) * 512]), w=[f"pws{w_}"], lane=f"pws{w_}")
            w_ = cur["w"]
            pbk = (7, 0, 6)[P.nxt("eb", 3)]
            for c in range(2):
                P.op("pe", lambda e: e.matmul(ps[pbk][:], lhsT=pT[:, c, i * 128:(i + 1) * 128], rhs=pws[w_][:, c, :], start=(c == 0), stop=(c == 1)),
                     r=[f"pT{i}", f"pws{w_}"], w=[f"ps{pbk}"])
            f = P.nxt("fst", 2)
            P.op("act", lambda e: e.activation(out=fst[f], in_=ps[bank][:], func=AF.Sigmoid, scale=rs(i)), r=[f"ps{bank}"] + rsdep(), w=[f"fst{f}"])
            P.op("dve", lambda e: e.tensor_tensor(out=fst[f], in0=ps[pbk][:], in1=fst[f], op=ALU.mult), r=[f"ps{pbk}", f"fst{f}"], w=[f"fst{f}"])
            r_ = P.nxt("rsb", 2)
            P.op("sp", lambda e: e.dma_start(out=rsb[r_], in_=Hsrc[lt][:, cbi * 512:(cbi + 1) * 512]), r=[(hsname, lt, cbi)], w=[f"rsb{r_}"], lane=f"rsb{r_}")
            P.op("dve", lambda e: e.tensor_tensor(out=rsb[r_], in0=rsb[r_], in1=fst[f], op=ALU.add), r=[f"rsb{r_}", f"fst{f}"], w=[f"rsb{r_}"])
            P.op("sp", lambda e: e.dma_start(out=Hdst[lt][:, cbi * 512:(cbi + 1) * 512], in_=rsb[r_]), r=[f"rsb{r_}"], w=[(hdname, lt, cbi)], lane=f"rsb{r_}s")
        dense(len(tiles), gate_w[layer], CB8, epi)

    def rsb_f32_stage(w_):
        return dn["pwf"][w_]

    L0_BLOCKS = [[0, 1, 2, 3, 4], [5, 6, 7, 8], [9, 10, 11, 12], [13, 14, 15, 16]]
    L1_BLOCKS = [[0, 1, 2, 3], [4, 5, 6, 7], [8, 9, 10, 11], [12, 13, 14, 15]]

    P.barrier()
    dense_layout()
    for blk in L0_BLOCKS:
        outproj_pass([(lt, xbuf[Q0 + lt]) for lt in blk], MIXT, "MIXT", 4, a_w_out, lambda lt, c: [], H1, "H1")
    load_gain(1)
    pipe = NormPipe(L0_BLOCKS, lambda lt: (H1[lt], [("H1", lt, c) for c in range(8)]))
    for bi, blk in enumerate(L0_BLOCKS):
        ple_pass(0, pipe, bi, blk, H1, "H1", H2, "H2", lambda lt: lt)
    if stop < 5:
        return finish_early()

    load_gain(2)
    pipe = NormPipe(L0_BLOCKS, lambda lt: (H2[lt], [("H2", lt, c) for c in range(8)]))
    for bi, blk in enumerate(L0_BLOCKS):
        pipe.block(bi)

        def epi(cbi, i, bank):
            lt = blk[i]
            tile = Q0 + lt
            if cbi == 0:
                def store(stg, key):
                    P.op("sp", lambda e: e.dma_start(out=KT1[:, :, lt * 128:(lt + 1) * 128].rearrange("g p t -> p g t"), in_=stg[0:64, 0:8, :]), r=[key], w=[("KT1", lt)], lane=key + "a")
                epi_T(bank, tile, 8, 64, "B", False, True, store, i)
            else:
                v1st = dn["v1st"]
                v = P.nxt("v1st", 2)
                P.op("act", lambda e: e.activation(out=v1st[v][:, :, 0:64], in_=ps[bank][:].rearrange("p (h d) -> p h d", h=8), func=AF.Copy, scale=rs(i)), r=[f"ps{bank}"] + rsdep(), w=[f"v1st{v}"])
                P.op("dve", lambda e: e.tensor_copy(out=v1st[v][:, :, 64:65], in_=kval[:, tile, :].unsqueeze(2)), r=["kval", f"v1st{v}"], w=[f"v1st{v}"])
                P.op("sp", lambda e: e.dma_start(out=V1[lt], in_=v1st[v]), r=[f"v1st{v}"], w=[("V1", lt)], lane=f"v1st{v}")
        dense(len(blk), w_kv, [(0, 512), (512, 512)], epi)

    load_gain(3)
    pipe = NormPipe(L1_BLOCKS, lambda l1: (H2[l1 + 1], [("H2", l1 + 1, c) for c in range(8)]))
    for bi, blk in enumerate(L1_BLOCKS):
        pipe.block(bi)

        def epi(cbi, i, bank):
            l1 = blk[i]
            tile = 16 + l1
            if cbi < 8:
                def store(stg, key):
                    P.op("sp", lambda e: e.dma_start(out=QT1[l1, cbi], in_=stg[0:64, 0:8, :]), r=[key], w=[("QT1", l1, cbi)], lane=key + "a")
                epi_T(bank, tile, 8, 64, "B", False, True, store, i)
            else:
                zc = cbi - 8
                epi_F(bank, AF.Silu, 512, Z1[l1][:, zc * 512:(zc + 1) * 512], ("Z1", l1, zc), i)
        dense(len(blk), b_w_in, [(512 * k, 512) for k in range(16)], epi)
    if stop < 6:
        return finish_early()

    P.barrier()
    AR.reset()
    kt1 = AR.alloc([64, 8, NQ * 128], BF16)
    v1b = AR.alloc([128, NQ, 8 * 65], BF16)
    esk = AR.alloc([128, 64], F32)
    q1 = [AR.alloc([64, 8, 128], BF16) for _ in range(2)]
    z1b = [AR.alloc([128, D], F32) for _ in range(2)]
    mix1 = AR.alloc([128, D], F32)
    mix1h = AR.alloc([128, D], BF16)
    pbh = [AR.alloc([128, 4, 128], BF16) for _ in range(6)]
    tst3 = [AR.alloc([128, 8, 128], BF16) for _ in range(2)]
    o1b = [AR.alloc([128, 2, 260], F32) for _ in range(2)]
    P.op("sp", lambda e: e.dma_start(out=kt1, in_=KT1.rearrange("g p t -> p g t")), r=[("KT1", lt) for lt in range(NQ)], w=["kt1"], lane="kt1")
    P.op("sp", lambda e: e.dma_start(out=v1b, in_=V1.rearrange("n p g d -> p n (g d)")), r=[("V1", lt) for lt in range(NQ)], w=["v1b"], lane="v1b")
    P.op("sp", lambda e: e.dma_start(out=esk, in_=sinks.partition_broadcast(128)), w=["esk"], lane="esk")
    P.op("act", lambda e: e.activation(out=esk, in_=esk, func=AF.Exp), r=["esk"], w=["esk"])

    def ov1(h):
        return ps[4 + h // 4][:, (h % 4) * 65:(h % 4) * 65 + 65]

    def swa_s1(g, qs, lt, hf, mask):
        sb = P.nxt("sbk", 4)
        P.op("pe", lambda e: e.matmul(ps[sb][:], lhsT=kt1[:, g, lt * 128:(lt + 1) * 128], rhs=q1[qs][:, 4 * hf:4 * hf + 4, :], start=True, stop=True),
             r=["kt1", f"q1{qs}"], w=[f"ps{sb}"])
        p_ = P.nxt("pbh", 6)
        P.op("act", lambda e: e.activation(out=pbh[p_], in_=ps[sb][:].rearrange("p (h t) -> p h t", h=4), func=AF.Exp, scale=SC_B),
             r=[f"ps{sb}"], w=[f"pbh{p_}"])
        mk, mname = mask
        P.op("dve", lambda e: e.tensor_tensor(out=pbh[p_], in0=pbh[p_], in1=mk.unsqueeze(1).broadcast_to([128, 4, 128]), op=ALU.mult), r=[f"pbh{p_}", mname], w=[f"pbh{p_}"])
        return p_

    for l1 in range(NQ1):
        zq = P.nxt("z1b", 2)
        P.op("sp", lambda e: e.dma_start(out=z1b[zq], in_=Z1[l1]), r=[("Z1", l1, c) for c in range(8)], w=[f"z1b{zq}"], lane=f"z1b{zq}")
        for g in range(8):
            qs = P.nxt("q1", 2)
            P.op("sp", lambda e: e.dma_start(out=q1[qs], in_=QT1[l1, g]), r=[("QT1", l1, g)], w=[f"q1{qs}"], lane=f"q1{qs}")
            steps = []
            for n_, (lt, mask) in enumerate(((l1, (mgt[:], "mgt")), (l1 + 1, (mle[:], "mle")))):
                for hf in range(2):
                    steps.append((swa_s1(g, qs, lt, hf, mask), hf, lt, n_))
            for (p_, hf, lt, n_) in steps:
                for hh in range(4):
                    h = 4 * hf + hh
                    P.op("pe", lambda e: e.matmul(ov1(h), lhsT=pbh[p_][:, hh, :], rhs=v1b[:, lt, g * 65:(g + 1) * 65], start=(n_ == 0 and h % 4 == 0), stop=False, skip_group_check=True),
                         r=[f"pbh{p_}", "v1b"], w=[f"ps{4 + h // 4}"])
            ob = P.nxt("o1b", 2)
            for bi in range(2):
                P.op("act", lambda e: e.copy(out=o1b[ob][:, bi, :], in_=ps[4 + bi][:, 0:260]), r=[f"ps{4 + bi}"], w=[("o1b", ob, bi)])
            for bi in range(2):
                P.op("dve", lambda e: e.tensor_tensor(out=sm[:, 32 * ob + 4 * bi:32 * ob + 4 * bi + 4], in0=o1b[ob][:, bi, :].rearrange("p (h d) -> p h d", d=65)[:, :, 64], in1=esk[:, g * 8 + 4 * bi:g * 8 + 4 * bi + 4], op=ALU.add),
                     r=[("o1b", ob, bi), "esk"], w=[("smrs", ob, bi)])
            P.op("dve", lambda e: e.reciprocal(out=sm[:, 32 * ob + 8:32 * ob + 16], in_=sm[:, 32 * ob:32 * ob + 8]), r=[("smrs", ob, 0), ("smrs", ob, 1)], w=[("sm_ri", ob)])
            for bi in range(2):
                P.op("dve", lambda e: e.tensor_tensor(out=mix1[:, g * 512 + bi * 256:g * 512 + bi * 256 + 256].rearrange("p (h d) -> p h d", h=4),
                                                       in0=o1b[ob][:, bi, :].rearrange("p (h d) -> p h d", d=65)[:, :, 0:64],
                                                       in1=sm[:, 32 * ob + 8 + 4 * bi:32 * ob + 8 + 4 * bi + 4].unsqueeze(2).broadcast_to([128, 4, 64]), op=ALU.mult),
                     r=[("o1b", ob, bi), ("sm_ri", ob)], w=[("mix1", g, bi)])
            P.op("pool", lambda e: e.tensor_tensor(out=mix1h[:, g * 512:(g + 1) * 512], in0=mix1[:, g * 512:(g + 1) * 512], in1=z1b[zq][:, g * 512:(g + 1) * 512], op=ALU.mult),
                 r=[("mix1", g, 0), ("mix1", g, 1), f"z1b{zq}"], w=[("mix1h", g)])
        for q in range(4):
            for jj in range(8):
                kc = q * 8 + jj
                P.op("pe", lambda e: e.transpose(out=psb[7][:, jj * 128:(jj + 1) * 128], in_=mix1h[:, kc * 128:(kc + 1) * 128], identity=idt[:]),
                     r=[("mix1h", kc // 4), "idt"], w=["ps7"])
            st_ = P.nxt("tst3", 2)
            P.op("dve", lambda e: e.tensor_copy(out=tst3[st_], in_=psb[7][:].rearrange("p (j c) -> p j c", j=8)), r=["ps7"], w=[f"tst3{st_}"])
            P.op("sp", lambda e: e.dma_start(out=MIXT1[l1][:, q * 8:(q + 1) * 8, :], in_=tst3[st_]), r=[f"tst3{st_}"], w=[("MIXT1", l1, q)], lane=f"tst3{st_}")
    if stop < 7:
        return finish_early()

    P.barrier()
    dense_layout()
    for blk in L1_BLOCKS:
        outproj_pass([(l1, H2[l1 + 1]) for l1 in blk], MIXT1, "MIXT1", 4, b_w_out, lambda l1, c: [("H2", l1 + 1, c)], H3, "H3")
    load_gain(4)
    pipe = NormPipe(L1_BLOCKS, lambda l1: (H3[l1], [("H3", l1, c) for c in range(8)]))
    for bi, blk in enumerate(L1_BLOCKS):
        ple_pass(1, pipe, bi, blk, H3, "H3", H4, "H4", lambda l1: l1 + 1)
    load_gain(5)
    fin = []
    for l1 in range(NQ1):
        s = P.nxt("xb", 2)
        xb_, xs, gbc = dn["xb"][s], dn["xs"][s], dn["gbc"]
        P.op("sp", lambda e: e.dma_start(out=xb_, in_=H4[l1]), r=[("H4", l1, c) for c in range(8)], w=[f"xb{s}"], lane=f"xb{s}")
        P.op("act", lambda e: e.activation(out=xs, in_=xb_, func=AF.Square, accum_out=ss[:, 4 * s:4 * s + 1]), r=[f"xb{s}"], w=[f"xs{s}", f"ss0{s}"])
        P.op("dve", lambda e: e.tensor_scalar(out=ss[:, 4 * s + 1:4 * s + 2], in0=ss[:, 4 * s:4 * s + 1], scalar1=1.0 / D, scalar2=1e-6, op0=ALU.mult, op1=ALU.add), r=[f"ss0{s}"], w=[f"ss1{s}"])
        P.op("act", lambda e: e.activation(out=ss[:, 4 * s + 3:4 * s + 4], in_=ss[:, 4 * s + 1:4 * s + 2], func=AF.Sqrt), r=[f"ss1{s}"], w=[f"ss3{s}"])
        P.op("dve", lambda e: e.reciprocal(out=ss[:, 4 * s + 2:4 * s + 3], in_=ss[:, 4 * s + 3:4 * s + 4]), r=[f"ss3{s}"], w=[f"ss2{s}"])
        P.op("dve", lambda e: e.scalar_tensor_tensor(out=xb_, in0=xb_, scalar=ss[:, 4 * s + 2:4 * s + 3], in1=gbc, op0=ALU.mult, op1=ALU.mult),
             r=[f"xb{s}", f"ss2{s}", "gbc"], w=[f"xb{s}"])
        fin.append(P.op("sp", lambda e: e.dma_start(out=out[l1], in_=xb_), r=[f"xb{s}"], w=[("out", l1)], lane=f"xb{s}s"))
    P.emit(final_waits=fin)
    return nc, P


def _consts(half):
    off = 0 if half == 1 else 2048
    tb = np.arange(S)
    ttrue = tb - off
    valid = (ttrue >= 0).astype(np.float32)
    c = {}
    c["c_ident"] = np.eye(128, dtype=np.float32).astype(NPBF)
    k = np.arange(128)[:, None]
    t = np.arange(128)[None, :]
    c["c_mle"] = (k <= t).astype(np.float32).astype(NPBF)
    c["c_mgt"] = (k > t).astype(np.float32).astype(NPBF)
    em = np.zeros((64, S), np.float32)
    em[tb // 64, tb] = 1.0
    c["c_emat"] = em.astype(NPBF)
    n = np.arange(256)
    jj = np.arange(64)
    m = ((16 * n[:, None] < 64 * jj[None, :] + 64) & (16 * n[:, None] + 32 > 64 * jj[None, :])).astype(np.float32)
    m[255] = 0
    c["c_m2s"] = np.ascontiguousarray(m.reshape(2, 128, 64).transpose(1, 0, 2)).astype(NPBF)
    kv = valid.reshape(NTILE, 128).T
    c["c_kval"] = np.ascontiguousarray(np.repeat(kv[:, :, None], 8, axis=2)).astype(NPBF)

    def rope_tab(half_dim):
        inv = (500000.0 ** (-np.arange(half_dim, dtype=np.float32) / half_dim)).astype(np.float32)
        ang = ttrue.astype(np.float32)[:, None] * inv[None, :]
        cs = np.cos(ang).astype(np.float32).reshape(NTILE, 128, half_dim).transpose(1, 0, 2)
        sn = np.sin(ang).astype(np.float32).reshape(NTILE, 128, half_dim).transpose(1, 0, 2)
        return np.ascontiguousarray(cs), np.ascontiguousarray(sn)
    c["c_cosA"], c["c_sinA"] = rope_tab(16)
    c["c_cosB"], c["c_sinB"] = rope_tab(8)
    keep = np.zeros((128, NQ, 64), np.float32)
    add = np.zeros((128, NQ, 64), np.float32)
    cm = np.zeros((128, NQ, 2, 128), np.float32)
    j0 = off // 64
    for qt in range(NQ):
        tq = (Q0 + qt) * 128 + np.arange(128)
        dist = (tq // 64)[:, None] - jj[None, :]
        causal = (dist >= 0) & (jj[None, :] >= j0)
        forced = (jj[None, :] == j0) | ((dist >= 0) & (dist < 2))
        kp = causal & ~forced
        ad = np.where(jj[None, :] == j0, 1e9, 0.0) + np.where(dist == 1, 2e9, 0.0) + np.where(dist == 0, 4e9, 0.0)
        ad = np.where(causal, np.where(forced, ad, 0.0), -1.0)
        keep[:, qt, :] = kp.astype(np.float32)
        add[:, qt, :] = ad
        nvalid = (16 * n[:, None] + 31 <= tq[None, :]) & (16 * n[:, None] >= off) & (n[:, None] < 255)
        cm[:, qt, :, :] = nvalid.astype(np.float32).reshape(2, 128, 128).transpose(1, 0, 2)
    c["c_keep"] = keep
    c["c_add"] = add
    c["c_cmask"] = cm.astype(NPBF)
    return c


_CACHE = {}


def _prep(inputs):
    f = lambda a: np.ascontiguousarray(np.asarray(a, dtype=np.float32))
    x = f(inputs["x"])
    p = f(inputs["p"])
    shared = {
        "a_w_in": f(inputs["a_w_in"])[0], "a_w_out": f(inputs["a_w_out"])[0],
        "w1k": f(inputs["a_cmp_w1_k"])[0], "w1v": f(inputs["a_cmp_w1_v"])[0],
        "w2k": f(inputs["a_cmp_w2_k"])[0], "w2v": f(inputs["a_cmp_w2_v"])[0],
        "posTk": np.ascontiguousarray(f(inputs["a_cmp_pos_k"])[0].T), "posTv": np.ascontiguousarray(f(inputs["a_cmp_pos_v"])[0].T),
        "w_kv": f(inputs["w_kv"]), "b_w_in": f(inputs["b_w_in"])[0], "b_w_out": f(inputs["b_w_out"])[0],
        "gate_w": f(inputs["ple_gate_w"]), "ple_proj": f(inputs["ple_proj"]),
        "gains": np.ascontiguousarray(np.stack([f(inputs["a_norm"])[0], f(inputs["ple_norm"])[0], f(inputs["kv_norm"]), f(inputs["b_norm"])[0],
                                                f(inputs["ple_norm"])[1], f(inputs["final_norm"])], 0)),
        "sinks": f(inputs["b_sinks"])[0],
    }
    consts = [_consts(0), _consts(1)]
    in_maps = []
    for c in range(8):
        b, half = c // 2, c % 2
        if half == 1:
            xb_ = x[b]
            pb_ = p[:, b, Q0 * 128:, :]
        else:
            xb_ = np.concatenate([np.zeros((2048, D), np.float32), x[b, :2048]], 0)
            pb_ = np.concatenate([np.zeros((2, 128, 256), np.float32), p[:, b, :2048, :]], 1)
        m = dict(shared)
        m.update(consts[half])
        m["xbuf"] = np.ascontiguousarray(xb_.reshape(NTILE, 128, D))
        m["pbuf"] = np.ascontiguousarray(pb_.reshape(2, NQ, 128, 256))
        in_maps.append(m)
    return in_maps


def kernel(**inputs):
    in_maps = _prep(inputs)
    if "nc" not in _CACHE:
        _CACHE["nc"] = build()[0]
    res = run_bass_kernel_spmd(_CACHE["nc"], in_maps, core_ids=list(range(8)))
    outp = np.zeros((4, S, D), np.float32)
    for c in range(8):
        b, half = c // 2, c % 2
        o = np.asarray(res.results[c]["out"]).reshape(2048, D)
        outp[b, half * 2048:(half + 1) * 2048] = o
    return outp
```
